# Optimizing a Trainium2 kernel written in Bass

```python
import math
import jax, jax.numpy as jnp
from jax import lax
import numpy as np

D_MODEL = 1024
BATCH = 8
SEQ = 4096
DEPTH = 4

N_MEM = 256
MIX_WIDTH = D_MODEL
N_GROUPS = 4
GROUP_WIDTH = MIX_WIDTH // N_GROUPS
CONF_WIDTH = 31
HG_HEADS = 4
HG_DK = GROUP_WIDTH // HG_HEADS
HG_DV = GROUP_WIDTH // HG_HEADS
HG_CHUNK = 64
SC_WIDTH = 3
MLA_HEADS = 4
MLA_NOPE = 64
MLA_ROPE = 32
MLA_V = GROUP_WIDTH // MLA_HEADS
MLA_Q_LORA = 256
MLA_KV_LORA = 128
ROPE_BASE = 10000.0
Q_BLOCK = 128
XA_HEADS = 4
XA_HEAD_DIM = 64
D_FF = -(-8 * D_MODEL // (3 * 256)) * 256
EPS = 1e-6
A_COLS = 2 * GROUP_WIDTH
B_COLS = 5 * GROUP_WIDTH
C_COLS = 3 * GROUP_WIDTH
D_COLS = MLA_Q_LORA + MLA_KV_LORA + MLA_ROPE
IN_COLS = A_COLS + B_COLS + C_COLS + D_COLS
SPLITS = [A_COLS, A_COLS + B_COLS, A_COLS + B_COLS + C_COLS]

kernel_name = "hybrid_parallel_group_encoder"


def rms_norm(x, g):
    xf = x.astype(jnp.float32)
    y = xf * lax.rsqrt(jnp.mean(xf * xf, axis=-1, keepdims=True) + EPS)
    return (y * g.astype(jnp.float32)).astype(x.dtype)


def layer_norm(x, g, b):
    xf = x.astype(jnp.float32)
    mu = jnp.mean(xf, axis=-1, keepdims=True)
    xc = xf - mu
    y = xc * lax.rsqrt(jnp.mean(xc * xc, axis=-1, keepdims=True) + EPS)
    return (y * g.astype(jnp.float32) + b.astype(jnp.float32)).astype(x.dtype)


def depthwise_conv(x, w, b):
    c = x.shape[-1]
    y = lax.conv_general_dilated(x, w[:, None, :].astype(x.dtype), window_strides=(1,),
                                 padding='SAME', dimension_numbers=('NWC', 'WIO', 'NWC'),
                                 feature_group_count=c)
    return y + b.astype(x.dtype)


def rope_tables(positions):
    inv = ROPE_BASE ** (-jnp.arange(0, MLA_ROPE, 2, dtype=jnp.float32) / MLA_ROPE)
    ang = positions.astype(jnp.float32)[..., None] * inv
    return jnp.cos(ang)[:, :, None, :], jnp.sin(ang)[:, :, None, :]


def apply_rope(x, cos, sin):
    x1, x2 = jnp.split(x, 2, axis=-1)
    return jnp.concatenate([x1 * cos - x2 * sin, x2 * cos + x1 * sin], axis=-1).astype(x.dtype)


def blocked_attention(q, k, v):
    b, s, h, d = q.shape
    scale = d ** -0.5
    qb = q.reshape(b, s // Q_BLOCK, Q_BLOCK, h, d).transpose(1, 0, 2, 3, 4)

    def one_block(qblk):
        sc = jnp.einsum('bqhd,bkhd->bhqk', qblk, k).astype(jnp.float32) * scale
        p = jax.nn.softmax(sc, axis=-1)
        return jnp.einsum('bhqk,bkhe->bqhe', p.astype(v.dtype), v)

    out = lax.map(one_block, qb)
    return out.transpose(1, 0, 2, 3, 4).reshape(b, s, h, v.shape[-1])


def conformer_conv(ua, dw_w, dw_b, ln_g, ln_b):
    a, gate = jnp.split(ua, 2, axis=-1)
    h = a * jax.nn.sigmoid(gate)
    h = depthwise_conv(h, dw_w, dw_b)
    h = layer_norm(h, ln_g, ln_b)
    return jax.nn.silu(h)


def hgrn2_bidir(ub, lb_fwd, lb_bwd, onorm_g):
    b, s, _ = ub.shape
    f32 = jnp.float32
    q, zf, zb, v, g = jnp.split(ub, 5, axis=-1)

    def log_forget(z, lb):
        zf32 = z.astype(f32)
        return jnp.log(jax.nn.sigmoid(zf32) + lb * jax.nn.sigmoid(-zf32))

    rev = lambda t: t[:, ::-1]
    logf = jnp.stack([log_forget(zf, lb_fwd), log_forget(rev(zb), lb_bwd)])
    qd = jnp.stack([q, rev(q)]).astype(f32)
    vd = jnp.stack([v, rev(v)]).astype(f32)
    n_chunks = s // HG_CHUNK

    def to_chunks(t, d):
        return t.reshape(2, b, n_chunks, HG_CHUNK, HG_HEADS, d).transpose(2, 0, 1, 4, 3, 5)

    mask = jnp.tril(jnp.ones((HG_CHUNK, HG_CHUNK), dtype=f32))[..., None]

    def step(state, inp):
        qc, lfc, vc = inp
        bcum = jnp.cumsum(lfc, axis=-2)
        kc = -jnp.expm1(lfc)
        o_inter = jnp.einsum('zbhcd,zbhde->zbhce', qc * jnp.exp(bcum), state)
        diff = bcum[..., :, None, :] - bcum[..., None, :, :]
        decay = jnp.exp(diff * mask) * mask
        attn = jnp.sum(qc[..., :, None, :] * kc[..., None, :, :] * decay, axis=-1)
        o_intra = jnp.einsum('zbhts,zbhse->zbhte', attn, vc)
        blast = bcum[..., -1:, :]
        new_state = jnp.exp(blast[..., 0, :])[..., :, None] * state + \
            jnp.einsum('zbhsd,zbhse->zbhde', kc * jnp.exp(blast - bcum), vc)
        return new_state, o_inter + o_intra

    init = jnp.zeros((2, b, HG_HEADS, HG_DK, HG_DV), f32)
    _, ys = lax.scan(step, init, (to_chunks(qd, HG_DK), to_chunks(logf, HG_DK), to_chunks(vd, HG_DV)))
    ys = ys.transpose(1, 2, 0, 4, 3, 5).reshape(2, b, s, HG_HEADS, HG_DV)
    o = ys[0] + ys[1][:, ::-1]
    o = rms_norm(o, onorm_g.reshape(HG_HEADS, HG_DV)) * \
        jax.nn.silu(g.astype(f32)).reshape(b, s, HG_HEADS, HG_DV)
    return o.reshape(b, s, GROUP_WIDTH).astype(ub.dtype)


def short_gated_conv(uc, dw_w, dw_b):
    gb, gc, xin = jnp.split(uc, 3, axis=-1)
    return gb * depthwise_conv(gc * xin, dw_w, dw_b)


def mla(ud, cos, sin, qa_g, wuq, kva_g, wukv, qn_g, kn_g):
    b, s, _ = ud.shape
    cq, ckv, kr = jnp.split(ud, [MLA_Q_LORA, MLA_Q_LORA + MLA_KV_LORA], axis=-1)
    q = (rms_norm(cq, qa_g) @ wuq).reshape(b, s, MLA_HEADS, MLA_NOPE + MLA_ROPE)
    kv = (rms_norm(ckv, kva_g) @ wukv).reshape(b, s, MLA_HEADS, MLA_NOPE + MLA_V)
    k_nope, v = jnp.split(kv, [MLA_NOPE], axis=-1)
    k = jnp.concatenate([k_nope, jnp.broadcast_to(kr[:, :, None, :], (b, s, MLA_HEADS, MLA_ROPE))], axis=-1)
    q = rms_norm(q, qn_g)
    k = rms_norm(k, kn_g)
    q = jnp.concatenate([q[..., :MLA_NOPE], apply_rope(q[..., MLA_NOPE:], cos, sin)], axis=-1)
    k = jnp.concatenate([k[..., :MLA_NOPE], apply_rope(k[..., MLA_NOPE:], cos, sin)], axis=-1)
    return blocked_attention(q, k, v).reshape(b, s, MLA_HEADS * MLA_V)


def setup_inputs(seed: int = 0) -> dict:
    key = jax.random.key(seed)
    ks = iter(jax.random.split(key, 40))
    L = DEPTH
    f32 = jnp.float32

    def nrm(shape, fan_in):
        return jax.random.normal(next(ks), shape, f32) * fan_in ** -0.5

    def gain(shape):
        return 1.0 + 0.01 * jax.random.normal(next(ks), shape, f32)

    def bias(shape):
        return 0.01 * jax.random.normal(next(ks), shape, f32)

    x = jax.random.normal(next(ks), (BATCH, SEQ, D_MODEL), f32)
    mem = jax.random.normal(next(ks), (BATCH, N_MEM, D_MODEL), f32)
    positions = jnp.arange(SEQ, dtype=jnp.int32)[None, :] + \
        jax.random.randint(next(ks), (BATCH, 1), 0, 1024, dtype=jnp.int32)
    return {
        "x": x,
        "mem": mem,
        "positions": positions,
        "g_mix": gain((L, D_MODEL)),
        "w_in": nrm((L, D_MODEL, IN_COLS), D_MODEL),
        "a_dw_w": nrm((L, CONF_WIDTH, GROUP_WIDTH), CONF_WIDTH),
        "a_dw_b": bias((L, GROUP_WIDTH)),
        "a_ln_g": gain((L, GROUP_WIDTH)),
        "a_ln_b": bias((L, GROUP_WIDTH)),
        "h_gamma": 0.5 * jax.random.normal(next(ks), (2, L, GROUP_WIDTH), f32),
        "h_onorm_g": gain((L, GROUP_WIDTH)),
        "c_dw_w": nrm((L, SC_WIDTH, GROUP_WIDTH), SC_WIDTH),
        "c_dw_b": bias((L, GROUP_WIDTH)),
        "m_qa_g": gain((L, MLA_Q_LORA)),
        "m_wuq": nrm((L, MLA_Q_LORA, MLA_HEADS * (MLA_NOPE + MLA_ROPE)), MLA_Q_LORA),
        "m_kva_g": gain((L, MLA_KV_LORA)),
        "m_wukv": nrm((L, MLA_KV_LORA, MLA_HEADS * (MLA_NOPE + MLA_V)), MLA_KV_LORA),
        "m_qn_g": gain((L, MLA_NOPE + MLA_ROPE)),
        "m_kn_g": gain((L, MLA_NOPE + MLA_ROPE)),
        "g_branch": gain((L, MIX_WIDTH)),
        "w_out": nrm((L, MIX_WIDTH, D_MODEL), MIX_WIDTH),
        "g_xq": gain((L, D_MODEL)),
        "g_mem": gain((L, D_MODEL)),
        "x_wq": nrm((L, D_MODEL, XA_HEADS * XA_HEAD_DIM), D_MODEL),
        "x_wkv": nrm((L, D_MODEL, 2 * XA_HEADS * XA_HEAD_DIM), D_MODEL),
        "x_qn_g": gain((L, XA_HEAD_DIM)),
        "x_kn_g": gain((L, XA_HEAD_DIM)),
        "x_wo": nrm((L, XA_HEADS * XA_HEAD_DIM, D_MODEL), XA_HEADS * XA_HEAD_DIM),
        "g_ffn": gain((L, D_MODEL)),
        "f_w13": nrm((L, D_MODEL, 2 * D_FF), D_MODEL),
        "f_w2": nrm((L, D_FF, D_MODEL), D_FF),
    }


def reference(x, mem, positions, g_mix, w_in, a_dw_w, a_dw_b, a_ln_g, a_ln_b, h_gamma, h_onorm_g,
              c_dw_w, c_dw_b, m_qa_g, m_wuq, m_kva_g, m_wukv, m_qn_g, m_kn_g, g_branch, w_out,
              g_xq, g_mem, x_wq, x_wkv, x_qn_g, x_kn_g, x_wo, g_ffn, f_w13, f_w2):
    b, s, _ = x.shape
    n_mem = mem.shape[1]
    cos, sin = rope_tables(positions)
    p = jax.nn.softmax(h_gamma.astype(jnp.float32), axis=1)
    lower = jnp.cumsum(p, axis=1) - p[:, :1]
    for l in range(DEPTH):
        n = rms_norm(x, g_mix[l])
        u = n @ w_in[l]
        ua, ub, uc, ud = jnp.split(u, SPLITS, axis=-1)
        y_a = conformer_conv(ua, a_dw_w[l], a_dw_b[l], a_ln_g[l], a_ln_b[l])
        y_b = hgrn2_bidir(ub, lower[0, l], lower[1, l], h_onorm_g[l])
        y_c = short_gated_conv(uc, c_dw_w[l], c_dw_b[l])
        y_d = mla(ud, cos, sin, m_qa_g[l], m_wuq[l], m_kva_g[l], m_wukv[l], m_qn_g[l], m_kn_g[l])
        cat = jnp.concatenate([y_a, y_b, y_c, y_d], axis=-1).reshape(b, s, N_GROUPS, GROUP_WIDTH)
        cat = rms_norm(cat, g_branch[l].reshape(N_GROUPS, GROUP_WIDTH)).reshape(b, s, MIX_WIDTH)
        x = x + cat @ w_out[l]
        hq = rms_norm(x, g_xq[l])
        mn = rms_norm(mem, g_mem[l])
        q = (hq @ x_wq[l]).reshape(b, s, XA_HEADS, XA_HEAD_DIM)
        kv = (mn @ x_wkv[l]).reshape(b, n_mem, XA_HEADS, 2 * XA_HEAD_DIM)
        k, v = jnp.split(kv, 2, axis=-1)
        q = rms_norm(q, x_qn_g[l])
        k = rms_norm(k, x_kn_g[l])
        o = blocked_attention(q, k, v).reshape(b, s, XA_HEADS * XA_HEAD_DIM)
        x = x + o @ x_wo[l]
        h = rms_norm(x, g_ffn[l])
        a1, a3 = jnp.split(h @ f_w13[l], 2, axis=-1)
        x = x + (jax.nn.silu(a1) * a3) @ f_w2[l]
    return x
```

```python
import os
import numpy as np
import ml_dtypes
import concourse.bass as bass
import concourse.mybir as mybir
from concourse.bass_utils import run_bass_kernel_spmd

F32 = mybir.dt.float32
BF16 = mybir.dt.bfloat16
I32 = mybir.dt.int32
AF = mybir.ActivationFunctionType
ALU = mybir.AluOpType
AX = mybir.AxisListType


class Buf:
    __slots__ = ("name", "w", "r")

    def __init__(self, name):
        self.name = name
        self.w = {}
        self.r = {}


class Prog:
    NSLOT = {"sp": 8, "pool": 6, "act": 2}

    def __init__(self, nc):
        self.nc = nc
        self.ops = {"pe": [], "act": [], "dve": [], "pool": [], "sp": []}
        self.count = {k: 0 for k in self.ops}
        self.waited = {k: {} for k in self.ops}
        self.sems = {}
        self.dma_sems = {}
        self.dma_uses = {}
        self.dma_next = {}
        self._ctx = []

    def enter(self, cm):
        v = cm.__enter__()
        self._ctx.append(cm)
        return v

    def setup_sems(self):
        for k in self.ops:
            self.sems[k] = self.enter(self.nc.semaphore("s_" + k))
        for q, n in self.NSLOT.items():
            self.dma_sems[q] = [self.enter(self.nc.semaphore(f"d_{q}{i}")) for i in range(n)]
            self.dma_uses[q] = [0] * n
            self.dma_next[q] = 0

    def _deps(self, eng, reads, writes):
        deps = {}
        for b in reads:
            for sid, (s, v) in b.w.items():
                if sid not in deps or deps[sid][1] < v:
                    deps[sid] = (s, v)
        for b in writes:
            for d in (b.w, b.r):
                for sid, (s, v) in d.items():
                    if sid not in deps or deps[sid][1] < v:
                        deps[sid] = (s, v)
        waits = []
        wd = self.waited[eng]
        own = id(self.sems[eng])
        for sid, (s, v) in deps.items():
            if eng == "pe" and sid == own:
                continue
            if wd.get(sid, 0) >= v:
                continue
            wd[sid] = v
            waits.append((s, v))
        return waits

    def _commit(self, tok, reads, writes):
        sid = id(tok[0])
        for b in reads:
            b.r[sid] = tok
        for b in writes:
            b.w = {sid: tok}
            b.r = {}

    def op(self, eng, fn, reads=(), writes=()):
        waits = self._deps(eng, reads, writes)
        self.count[eng] += 1
        tok = (self.sems[eng], self.count[eng])
        self.ops[eng].append((waits, fn, (tok[0], 1)))
        self._commit(tok, reads, writes)

    def dma(self, q, out_ap, in_ap, reads=(), writes=(), **kw):
        eng = {"sp": "sp", "pool": "pool", "act": "act"}[q]
        slot = self.dma_next[q]
        self.dma_next[q] = (slot + 1) % len(self.dma_sems[q])
        sem = self.dma_sems[q][slot]
        waits = self._deps(eng, reads, writes)
        prev = self.dma_uses[q][slot] * 16
        if prev and self.waited[eng].get(id(sem), 0) < prev:
            self.waited[eng][id(sem)] = prev
            waits.append((sem, prev))
        self.dma_uses[q][slot] += 1
        tok = (sem, self.dma_uses[q][slot] * 16)
        self.ops[eng].append((waits, lambda e: e.dma_start(out=out_ap, in_=in_ap, **kw), (sem, 16)))
        self._commit(tok, reads, writes)

    def barrier(self):
        toks = [(self.sems[k], self.count[k]) for k in self.ops if self.count[k] > 0]
        for q, sems in self.dma_sems.items():
            for i, s in enumerate(sems):
                if self.dma_uses[q][i] > 0:
                    toks.append((s, self.dma_uses[q][i] * 16))
        for eng in self.ops:
            waits = []
            wd = self.waited[eng]
            for s, v in toks:
                if wd.get(id(s), 0) >= v:
                    continue
                wd[id(s)] = v
                waits.append((s, v))
            if waits:
                self.ops[eng].append((waits, None, None))

    def finish_wait_all(self, bufs):
        waits = self._deps("sp", bufs, [])
        self.ops["sp"].append((waits, None, None))

    def emit(self):
        nc = self.nc
        with nc.Block() as block:
            def run(eng_name):
                def f(e):
                    for waits, fn, inc in self.ops[eng_name]:
                        for s, v in waits:
                            e.wait_ge(s, v)
                        if fn is not None:
                            ins = fn(e)
                            if inc is not None:
                                ins.then_inc(inc[0], inc[1])
                return f
            block.tensor(run("pe"))
            block.scalar(run("act"))
            block.vector(run("dve"))
            block.gpsimd(run("pool"))
            block.sync(run("sp"))

    def close(self):
        for cm in reversed(self._ctx):
            cm.__exit__(None, None, None)
        self._ctx = []


D = 1024
KC = 8
NMEM = 256
GW = 256
INC = 2976
DFF = 2816
NJ = 22
EPS = 1e-6
HG_STAGE = int(os.environ.get('HG_STAGE', '9'))
CH = 32


def vec_layout(L):
    per = [("g_mix", 8), ("a_w", 62), ("a_b", 2), ("a_lng", 2), ("a_lnb", 2), ("onorm", 2),
           ("c_w", 6), ("c_b", 2), ("qa_g", 2), ("kva_g", 1), ("qn_g", 1), ("kn_g", 1),
           ("gbr", 6), ("gbrD", 4), ("g_xq", 8), ("g_mem", 8), ("xqn", 1), ("xkn", 1), ("g_ffn", 8)]
    glob = [("hgam", 4 * L), ("eps", 1), ("pi", 1), ("invf", 1), ("hpi", 1), ("hm0", 1), ("hm1", 1)]
    idx = {}
    c = 0
    for k, n in glob:
        idx[k] = c
        c += n
    for l in range(L):
        for k, n in per:
            idx[(k, l)] = c
            c += n
    return idx, c


def pack_vecs(inp, L):
    idx, n = vec_layout(L)
    V = np.zeros((128, n), np.float32)

    def put(key, arr):
        arr = np.asarray(arr, np.float32)
        c0 = idx[key]
        V[:arr.shape[1], c0:c0 + arr.shape[0]] = arr.T

    hg = np.asarray(inp["h_gamma"], np.float32)
    put("hgam", hg.reshape(2 * L * 2, 128))
    V[:, idx["eps"]] = EPS
    V[:, idx["pi"]] = np.pi
    V[:, idx["hpi"]] = np.pi / 2
    V[0:64, idx["hm0"]] = 1.0
    V[64:128, idx["hm1"]] = 1.0
    inv = (10000.0 ** (-np.arange(0, 32, 2, dtype=np.float32) / 32)).astype(np.float32)
    inv = (inv / np.float32(2 * np.pi)).astype(np.float32)
    V[64:80, idx["invf"]] = inv
    V[80:96, idx["invf"]] = inv
    for l in range(L):
        put(("g_mix", l), inp["g_mix"][l].reshape(8, 128))
        aw = np.asarray(inp["a_dw_w"][l])
        put(("a_w", l), aw.reshape(31, 2, 128).transpose(1, 0, 2).reshape(62, 128))
        put(("a_b", l), inp["a_dw_b"][l].reshape(2, 128))
        put(("a_lng", l), inp["a_ln_g"][l].reshape(2, 128))
        put(("a_lnb", l), inp["a_ln_b"][l].reshape(2, 128))
        put(("onorm", l), inp["h_onorm_g"][l].reshape(2, 128))
        cw = np.asarray(inp["c_dw_w"][l])
        put(("c_w", l), cw.reshape(3, 2, 128).transpose(1, 0, 2).reshape(6, 128))
        put(("c_b", l), inp["c_dw_b"][l].reshape(2, 128))
        put(("qa_g", l), inp["m_qa_g"][l].reshape(2, 128))
        put(("kva_g", l), inp["m_kva_g"][l].reshape(1, 128))
        put(("qn_g", l), inp["m_qn_g"][l].reshape(1, 96))
        put(("kn_g", l), inp["m_kn_g"][l].reshape(1, 96))
        gb = np.asarray(inp["g_branch"][l])
        put(("gbr", l), gb[:768].reshape(6, 128))
        put(("gbrD", l), gb[768:].reshape(4, 64))
        put(("g_xq", l), inp["g_xq"][l].reshape(8, 128))
        put(("g_mem", l), inp["g_mem"][l].reshape(8, 128))
        put(("xqn", l), inp["x_qn_g"][l].reshape(1, 64))
        put(("xkn", l), inp["x_kn_g"][l].reshape(1, 64))
        put(("g_ffn", l), inp["g_ffn"][l].reshape(8, 128))
    return V


def const_mats():
    c = {}
    c["ident"] = np.eye(128, dtype=np.float32)
    rt = np.zeros((128, 128), np.float32)
    for j in range(16):
        rt[80 + j, 64 + j] = -1.0
        rt[64 + j, 80 + j] = 1.0
    c["rt"] = rt
    sh = np.zeros((128, 128), np.float32)
    for k in range(32):
        sh[k, 64 + k] = 1.0
    c["sh"] = sh
    bd = np.zeros((128, 128), np.float32)
    bd[:64, :64] = 1.0
    bd[64:, 64:] = 1.0
    c["bd"] = bd
    s = np.arange(128)[:, None]
    t = np.arange(128)[None, :]
    same = (s // CH) == (t // CH)
    mf = (same & (s <= t)).astype(np.float32)
    mb = (same & (s >= t)).astype(np.float32)
    rm = np.ones((128, 512), np.float32)
    rm[:, ::CH] = 0.0
    c["rmask"] = rm
    c["mf"] = np.concatenate([mf, mf], axis=1)
    c["mb"] = np.concatenate([mb, mb], axis=1)
    return c


class T:
    __slots__ = ("h", "b")

    def __init__(self, h, name):
        self.h = h
        self.b = Buf(name)

    def __getitem__(self, k):
        return self.h[k]


class Rot:
    def __init__(self, tiles):
        self.tiles = tiles
        self.i = 0

    def get(self):
        t = self.tiles[self.i]
        self.i = (self.i + 1) % len(self.tiles)
        return t


def _b(lst):
    return [t.b for t in lst]


class KB:
    def __init__(self, S, L, TT=512, debug=False):
        self.S, self.L, self.TT = S, L, TT
        self.NT = S // TT
        self.debug = debug
        nc = bass.Bass("TRN2", target_bir_lowering=False)
        self.nc = nc
        self.P = Prog(nc)
        self.P.setup_sems()
        self.idx, self.NV = vec_layout(L)
        self._scopes = []
        self._n = 0
        din = lambda name, shape, dt: nc.dram_tensor(name, shape, dt, kind="ExternalInput").ap()
        self.xin = din("xT", [D, S], F32)
        self.memT = din("memT", [D, NMEM], F32)
        self.pos = din("pos", [1, S], I32)
        self.vecs_d = din("vecs", [128, self.NV], F32)
        self.cm_d = {k: din("c_" + k, list(v.shape), F32) for k, v in const_mats().items()}
        self.w_in = din("w_in", [L, D, INC], F32)
        self.wuq = din("m_wuq", [L, 256, 384], F32)
        self.wukv = din("m_wukv", [L, 128, 512], F32)
        self.w_out = din("w_out", [L, D, D], F32)
        self.x_wq = din("x_wq", [L, D, 256], F32)
        self.x_wkv = din("x_wkv", [L, D, 512], F32)
        self.x_wo = din("x_wo", [L, 256, D], F32)
        self.f_w13 = din("f_w13", [L, D, 2 * DFF], F32)
        self.f_w2 = din("f_w2", [L, DFF, D], F32)
        self.y = nc.dram_tensor("y", [D, S], F32, kind="ExternalOutput").ap()
        kind = "ExternalOutput" if debug else "Internal"
        self.scr = {}
        self.dbufs = {}
        for name, shape, dt in [("hA", [256, S], BF16), ("qh", [256, S], BF16), ("lf", [2, 256, S], F32),
                                ("sg", [256, S], BF16), ("vh", [S, 256], BF16), ("mC", [256, S], BF16),
                                ("gbC", [256, S], BF16), ("cq", [256, S], BF16), ("ckv", [128, S], BF16),
                                ("kr", [32, S], BF16), ("cat", [D, S], BF16), ("Cf", [96, S], F32),
                                ("Sf", [96, S], F32)]:
            self.scr[name] = nc.dram_tensor("s_" + name, shape, dt, kind=kind).ap()
            self.dbufs[name] = [Buf(f"{name}{t}") for t in range(self.NT)]
        self.dbufs["xin"] = [Buf(f"xin{t}") for t in range(self.NT)]
        self.dbufs["y"] = [Buf(f"y{t}") for t in range(self.NT)]
        self.wbuf = Buf("weights_dram")

    def _name(self, s):
        self._n += 1
        return f"{s}_{self._n}"

    def sb(self, name, shape, dt):
        cm = self.nc.sbuf_tensor(self._name(name), shape, dt)
        h = cm.__enter__()
        self._scopes[-1].append(cm)
        return T(h, name)

    def psb(self, name, shape, dt):
        cm = self.nc.psum_tensor(self._name(name), shape, dt)
        h = cm.__enter__()
        self._scopes[-1].append(cm)
        return T(h, name)

    def rot(self, name, shape, dt, n, psum=False):
        return Rot([(self.psb if psum else self.sb)(f"{name}{i}", shape, dt) for i in range(n)])

    def push(self):
        self._scopes.append([])

    def pop(self):
        self.P.barrier()
        self.P.emit()
        for k in self.P.ops:
            self.P.ops[k] = []
        for cm in reversed(self._scopes.pop()):
            cm.__exit__(None, None, None)

    def mm(self, out, lhsT, rhs, start, stop, reads, writes):
        self.P.op("pe", lambda e: e.matmul(out, lhsT, rhs, start=start, stop=stop), _b(reads), _b(writes))

    def tr(self, out, in_, ident, reads, writes):
        self.P.op("pe", lambda e: e.transpose(out, in_, ident), _b(reads), _b(writes))

    def act(self, out, in_, func, reads, writes, bias=None, scale=1.0):
        if bias is None:
            self.P.op("act", lambda e: e.activation(out, in_, func, scale=scale), _b(reads), _b(writes))
        else:
            self.P.op("act", lambda e: e.activation(out, in_, func, bias=bias, scale=scale), _b(reads), _b(writes))

    def tt(self, eng, out, a, b, op, reads, writes):
        self.P.op(eng, lambda e: e.tensor_tensor(out, a, b, op), _b(reads), _b(writes))

    def ts(self, eng, out, a, s1, s2, op0, op1, reads, writes):
        if s2 is None:
            self.P.op(eng, lambda e: e.tensor_scalar(out, a, s1, None, op0), _b(reads), _b(writes))
        else:
            self.P.op(eng, lambda e: e.tensor_scalar(out, a, s1, s2, op0, op1), _b(reads), _b(writes))

    def stt(self, eng, out, a, sc, b, op0, op1, reads, writes):
        self.P.op(eng, lambda e: e.scalar_tensor_tensor(out, a, sc, b, op0, op1), _b(reads), _b(writes))

    def cp(self, eng, out, in_, reads, writes):
        if eng == "act":
            self.P.op("act", lambda e: e.copy(out, in_), _b(reads), _b(writes))
        else:
            self.P.op(eng, lambda e: e.tensor_copy(out, in_), _b(reads), _b(writes))

    def ms(self, eng, ap, val, writes):
        self.P.op(eng, lambda e: e.memset(ap, val), [], _b(writes))

    def dma(self, q, out, in_, reads, writes):
        self.P.dma(q, out, in_, _b(reads), _b(writes))

    def vc(self, key, c=0, p0=0, p1=128):
        i = self.idx[key] + c
        return self.vecs[p0:p1, i:i + 1]

    def rstd(self, out, ps, n, reads, writes, p0=0, p1=128):
        self.act(out, ps, AF.Ln, reads + [self.vecs], writes, bias=self.vc("eps", 0, p0, p1), scale=1.0 / n)
        self.act(out, out, AF.Exp, writes, writes, scale=-0.5)

    def setup(self):
        L, S, TT = self.L, self.S, self.TT
        self.push()
        self.vecs = self.sb("vecs", [128, self.NV], F32)
        self.dma("sp", self.vecs[:], self.vecs_d, [], [self.vecs])
        cmf = {}
        for k in ["ident", "rt", "sh", "bd"]:
            cmf[k] = self.sb("cf_" + k, [128, 128], F32)
            self.dma("sp", cmf[k][:], self.cm_d[k], [], [cmf[k]])
        self.bd_f = cmf["bd"]
        self.ident = self.sb("ident", [128, 128], BF16)
        self.rt = self.sb("rt", [128, 128], BF16)
        self.sh = self.sb("sh", [128, 128], BF16)
        self.bd = self.sb("bd", [128, 128], BF16)
        for k, t in [("ident", self.ident), ("rt", self.rt), ("sh", self.sh), ("bd", self.bd)]:
            self.cp("dve", t[:], cmf[k][:], [cmf[k]], [t])
        self.mf = self.sb("mf", [128, 256], F32)
        self.mb = self.sb("mb", [128, 256], F32)
        self.dma("sp", self.mf[:], self.cm_d["mf"], [], [self.mf])
        self.dma("sp", self.mb[:], self.cm_d["mb"], [], [self.mb])
        self.ones = self.sb("ones", [128, 128], BF16)
        self.ms("dve", self.ones[:], 1.0, [self.ones])
        self.ones_f = self.sb("ones_f", [128, 64], F32)
        self.ms("dve", self.ones_f[:], 1.0, [self.ones_f])
        nh = 4 * L
        hg0 = self.idx["hgam"]
        self.lb = self.sb("lb", [128, nh], F32)
        self.oml = self.sb("oml", [128, nh], F32)
        ex = self.sb("hg_e", [128, nh], F32)
        sm = self.sb("hg_s", [128, 4], F32)
        self.act(ex[:], self.vecs[:, hg0:hg0 + nh], AF.Exp, [self.vecs], [ex])
        e4 = ex[:].rearrange("p (d l c) -> p d l c", d=2, l=L, c=2)
        lb4 = self.lb[:].rearrange("p (d l c) -> p d l c", d=2, l=L, c=2)
        s3 = sm[:].rearrange("p (d c) -> p d c", d=2, c=2)
        self.cp("dve", s3, e4[:, :, 0, :], [ex], [sm])
        for l in range(1, L):
            self.tt("dve", s3, s3, e4[:, :, l, :], ALU.add, [sm, ex], [sm])
        self.P.op("dve", lambda e: e.reciprocal(sm[:], sm[:]), [sm.b], [sm.b])
        self.ms("dve", self.lb[:], 0.0, [self.lb])
        for l in range(1, L):
            self.tt("dve", lb4[:, :, l, :], e4[:, :, l, :], s3, ALU.mult, [ex, sm, self.lb], [self.lb])
            self.tt("dve", lb4[:, :, l, :], lb4[:, :, l, :], lb4[:, :, l - 1, :], ALU.add, [self.lb], [self.lb])
        self.ts("dve", self.oml[:], self.lb[:], -1.0, 1.0, ALU.mult, ALU.add, [self.lb], [self.oml])
        self.push()
        pi_ = self.sb("pos_i", [96, TT], I32)
        pf = self.sb("pos_f", [96, TT], F32)
        ang = self.sb("ang", [96, TT], F32)
        r_cf = self.rot("cft", [96, TT], F32, 2)
        r_sf = self.rot("sft", [96, TT], F32, 2)
        twopi = float(2 * np.pi)
        ki = self.sb("ang_i", [96, TT], I32)
        kf = self.sb("ang_kf", [96, TT], F32)
        tq = self.sb("ang_t", [96, TT], F32)
        R_ = slice(64, 96)

        def reduce_turns(u):
            self.cp("dve", ki[R_, :], u[R_, :], [u], [ki])
            self.cp("dve", kf[R_, :], ki[R_, :], [ki], [kf])
            self.tt("dve", u[R_, :], u[R_, :], kf[R_, :], ALU.subtract, [u, kf], [u])
            self.ts("dve", tq[R_, :], u[R_, :], 0.5, None, ALU.is_gt, None, [u], [tq])
            self.tt("dve", u[R_, :], u[R_, :], tq[R_, :], ALU.subtract, [u, tq], [u])
            self.ts("dve", tq[R_, :], u[R_, :], -0.5, None, ALU.is_lt, None, [u], [tq])
            self.tt("dve", u[R_, :], u[R_, :], tq[R_, :], ALU.add, [u, tq], [u])

        for t in range(self.NT):
            sl = slice(t * TT, (t + 1) * TT)
            self.dma("sp", pi_[64:96, :], self.pos[0:1, sl].to_broadcast([32, TT]), [], [pi_])
            self.cp("dve", pf[R_, :], pi_[R_, :], [pi_], [pf])
            self.ts("dve", pf[R_, :], pf[R_, :], self.vc("invf", 0, 64, 96), None, ALU.mult, None, [pf, self.vecs], [pf])
            self.ts("dve", ang[R_, :], pf[R_, :], 0.25, None, ALU.add, None, [pf], [ang])
            cf = r_cf.get()
            sf = r_sf.get()
            self.ms("dve", cf[0:64, :], 1.0, [cf])
            self.ms("dve", sf[0:64, :], 0.0, [sf])
            reduce_turns(pf)
            self.act(sf[R_, :], pf[R_, :], AF.Sin, [pf], [sf], scale=twopi)
            reduce_turns(ang)
            self.act(cf[R_, :], ang[R_, :], AF.Sin, [ang], [cf], scale=twopi)
            self.P.dma("pool", self.scr["Cf"][:, sl], cf[:], [cf.b], [self.dbufs["Cf"][t]])
            self.P.dma("pool", self.scr["Sf"][:, sl], sf[:], [sf.b], [self.dbufs["Sf"][t]])
        self.pop()

    def load_x(self, xsrc, xname, t, xt):
        TT = self.TT
        src = xsrc.rearrange("(c p) s -> p c s", p=128)[:, :, t * TT:(t + 1) * TT]
        self.P.dma("sp", xt[:], src, [self.dbufs[xname][t]], [xt.b])

    def norm_tile(self, xt, gkey, l, sq, rs, nT, pst):
        TT = self.TT
        self.act(sq[:], xt[:], AF.Square, [xt], [sq])
        for c in range(KC):
            self.mm(pst[:, :TT], self.ones[:], sq[:, c, :], c == 0, c == KC - 1, [sq, self.ones], [pst])
        self.rstd(rs[:], pst[:, :TT], D, [pst], [rs])
        for c in range(KC):
            self.stt("dve", nT[:, c, :], xt[:, c, :], self.vc((gkey, l), c), rs[:], ALU.mult, ALU.mult,
                     [xt, rs, self.vecs], [nT])

    def group_norm_store(self, ys, gcols, row0, t, pst, r_sq, rs, r_ob, npart=128):
        TT = self.TT
        n = len(ys)
        for i, (yt, yap) in enumerate(ys):
            sq = r_sq.get()
            self.act(sq[0:npart, :], yap, AF.Square, [yt], [sq])
            self.mm(pst[0:npart, :TT], self.ones[0:npart, 0:npart], sq[0:npart, :], i == 0, i == n - 1,
                    [sq, self.ones], [pst])
        self.rstd(rs[0:npart, :], pst[0:npart, :TT], GW, [pst], [rs], 0, npart)
        for i, (yt, yap) in enumerate(ys):
            ob = r_ob.get()
            self.stt("dve", ob[0:npart, :], yap, gcols[i], rs[0:npart, :], ALU.mult, ALU.mult,
                     [yt, rs, self.vecs], [ob])
            r0 = row0 + i * npart
            self.P.dma("pool", self.scr["cat"][r0:r0 + npart, t * TT:(t + 1) * TT], ob[0:npart, :],
                       [ob.b], [self.dbufs["cat"][t]])

    def ph_inproj(self, l, xsrc, xname):
        TT, NT = self.TT, self.NT
        self.push()
        w = [self.sb(f"w_in{kc}", [128, INC], BF16) for kc in range(KC)]
        for kc in range(KC):
            self.P.dma("pool", w[kc][:], self.w_in[l, kc * 128:(kc + 1) * 128, :], [self.wbuf], [w[kc].b])
        r_xt = self.rot("xt", [128, KC, TT], F32, 2)
        r_sq = self.rot("sq", [128, KC, TT], BF16, 1)
        r_rs = self.rot("rs", [128, TT], F32, 2)
        r_nT = self.rot("nT", [128, KC, TT], BF16, 2)
        r_ps = self.rot("ps", [128, 512], F32, 6, psum=True)
        r_pst = self.rot("pst", [128, 512], F32, 1, psum=True)
        r_ob = self.rot("ob", [128, TT], BF16, 4)
        r_of = self.rot("of", [128, TT], F32, 4)
        r_vt = self.rot("vt", [128, 256], BF16, 2)
        sl_t = lambda t: slice(t * TT, (t + 1) * TT)
        lbc = lambda d, cc: ((d * self.L + l) * 2 + cc)

        def proj(col0, ncol, nT):
            ps = r_ps.get()
            for kc in range(KC):
                self.mm(ps[0:ncol, :TT], w[kc][:, col0:col0 + ncol], nT[:, kc, :], kc == 0, kc == KC - 1,
                        [w[kc], nT], [ps])
            return ps

        def store(name, row0, nrow, t, ob):
            dst = self.scr[name]
            self.P.dma("pool", dst[row0:row0 + nrow, sl_t(t)], ob[0:nrow, :], [ob.b], [self.dbufs[name][t]])

        for t in range(NT):
            xt = r_xt.get()
            self.load_x(xsrc, xname, t, xt)
            sq, rs, nT = r_sq.get(), r_rs.get(), r_nT.get()
            self.norm_tile(xt, "g_mix", l, sq, rs, nT, r_pst.get())
            for j in range(2):
                psg = proj((2 + j) * 128, 128, nT)
                sg = r_of.get()
                self.act(sg[:], psg[:, :TT], AF.Sigmoid, [psg], [sg])
                psa = proj(j * 128, 128, nT)
                ob = r_ob.get()
                self.tt("dve", ob[:], psa[:, :TT], sg[:], ALU.mult, [psa, sg], [ob])
                store("hA", j * 128, 128, t, ob)
            for j in range(2):
                ps = proj(512 + j * 128, 128, nT)
                ob = r_ob.get()
                self.cp("dve", ob[:], ps[:, :TT], [ps], [ob])
                store("qh", j * 128, 128, t, ob)
            for d in range(2):
                for j in range(2):
                    ps = proj(768 + d * 256 + j * 128, 128, nT)
                    of = r_of.get()
                    self.act(of[:], ps[:, :TT], AF.Sigmoid, [ps], [of])
                    of2 = r_of.get()
                    self.act(of2[:], of[:], AF.Ln, [of, self.oml, self.lb], [of2],
                             bias=self.lb[:, lbc(d, j):lbc(d, j) + 1], scale=self.oml[:, lbc(d, j):lbc(d, j) + 1])
                    self.P.dma("pool", self.scr["lf"][d, j * 128:(j + 1) * 128, sl_t(t)], of2[:], [of2.b],
                               [self.dbufs["lf"][t]])
            for j in range(2):
                ps = proj(1536 + j * 128, 128, nT)
                ob = r_ob.get()
                self.act(ob[:], ps[:, :TT], AF.Silu, [ps], [ob])
                store("sg", j * 128, 128, t, ob)
            for tb in range(TT // 128):
                ps = r_ps.get()
                for kc in range(KC):
                    self.mm(ps[:, 0:256], nT[:, kc, tb * 128:(tb + 1) * 128], w[kc][:, 1280:1536], kc == 0,
                            kc == KC - 1, [w[kc], nT], [ps])
                vt = r_vt.get()
                self.cp("act", vt[:], ps[:, 0:256], [ps], [vt])
                r0 = t * TT + tb * 128
                self.P.dma("pool", self.scr["vh"][r0:r0 + 128, :], vt[:], [vt.b], [self.dbufs["vh"][t]])
            for j in range(2):
                ps = proj(1792 + j * 128, 128, nT)
                ob = r_ob.get()
                self.cp("act", ob[:], ps[:, :TT], [ps], [ob])
                store("gbC", j * 128, 128, t, ob)
                psc = proj(2048 + j * 128, 128, nT)
                of = r_of.get()
                self.cp("act", of[:], psc[:, :TT], [psc], [of])
                psx = proj(2304 + j * 128, 128, nT)
                ob = r_ob.get()
                self.tt("dve", ob[:], psx[:, :TT], of[:], ALU.mult, [psx, of], [ob])
                store("mC", j * 128, 128, t, ob)
            for j in range(2):
                ps = proj(2560 + j * 128, 128, nT)
                ob = r_ob.get()
                self.cp("dve", ob[:], ps[:, :TT], [ps], [ob])
                store("cq", j * 128, 128, t, ob)
            ps = proj(2816, 128, nT)
            ob = r_ob.get()
            self.cp("act", ob[:], ps[:, :TT], [ps], [ob])
            store("ckv", 0, 128, t, ob)
            ps = proj(2944, 32, nT)
            ob = r_ob.get()
            self.cp("dve", ob[0:32, :], ps[0:32, :TT], [ps], [ob])
            store("kr", 0, 32, t, ob)
        self.pop()

    def ph_conv(self, l, kind, scoped=True):
        TT, NT, S = self.TT, self.NT, self.S
        W = 31 if kind == "A" else 3
        H = W // 2
        src_name = "hA" if kind == "A" else "mC"
        wkey, bkey = ("a_w", "a_b") if kind == "A" else ("c_w", "c_b")
        if scoped:
            self.push()
        dg = [self.sb(f"dg{i}", [128, 128], BF16) for i in range(2 * W)]
        for i in range(2 * W):
            self.ts("dve", dg[i][:], self.ident[:], self.vc((wkey, l), i), None, ALU.mult, None,
                    [self.ident, self.vecs], [dg[i]])
        r_win = self.rot("win", [128, TT + 2 * H], BF16, 4)
        r_ps = self.rot("ps", [128, 512], F32, 3 if scoped else 2, psum=True)
        r_pst = self.rot("pst", [128, 512], F32, 3 if scoped else 2, psum=True)
        cb = [self.rot(f"c32_{cc}", [128, TT], F32, 2) for cc in range(2)]
        yb = [self.rot(f"y32_{cc}", [128, TT], F32, 2) for cc in range(2)]
        r_tmp = self.rot("tmp", [128, TT], F32, 4)
        r_sq = self.rot("sqb", [128, TT], BF16, 2)
        r_rs = self.rot("rs", [128, TT], F32, 2)
        r_ob = self.rot("ob", [128, TT], BF16, 4)
        r_gb = self.rot("gb", [128, TT], BF16, 2)
        onesf = self.sb("onesf", [128, 128], F32)
        self.ms("dve", onesf[:], 1.0, [onesf])
        for t in range(NT):
            lo, hi = t * TT - H, t * TT + TT + H
            clo, chi = max(lo, 0), min(hi, S)
            rd = [self.dbufs[src_name][tt_] for tt_ in (t - 1, t, t + 1) if 0 <= tt_ < NT]
            cs = []
            for cc in range(2):
                win = r_win.get()
                if clo != lo or chi != hi:
                    self.ms("pool", win[:], 0.0, [win])
                self.P.dma("sp", win[:, clo - lo:chi - lo], self.scr[src_name][cc * 128:(cc + 1) * 128, clo:chi],
                           rd, [win.b])
                ps = r_ps.get()
                for tap in range(W):
                    self.mm(ps[:, :TT], dg[cc * W + tap][:], win[:, tap:tap + TT], tap == 0, tap == W - 1,
                            [dg[cc * W + tap], win], [ps])
                c32 = cb[cc].get()
                self.act(c32[:], ps[:, :TT], AF.Identity, [ps, self.vecs], [c32], bias=self.vc((bkey, l), cc))
                cs.append(c32)
            ys = []
            if kind == "A":
                psm, psq = r_pst.get(), r_pst.get()
                for cc in range(2):
                    self.mm(psm[:, :TT], onesf[:], cs[cc][:], cc == 0, cc == 1, [onesf, cs[cc]], [psm])
                for cc in range(2):
                    sq = r_tmp.get()
                    self.act(sq[:], cs[cc][:], AF.Square, [cs[cc]], [sq])
                    self.mm(psq[:, :TT], onesf[:], sq[:], cc == 0, cc == 1, [onesf, sq], [psq])
                mean = r_tmp.get()
                self.act(mean[:], psm[:, :TT], AF.Identity, [psm], [mean], scale=1.0 / GW)
                msq = r_tmp.get()
                self.tt("dve", msq[:], mean[:], mean[:], ALU.mult, [mean], [msq])
                var = r_rs.get()
                self.stt("dve", var[:], psq[:, :TT], 1.0 / GW, msq[:], ALU.mult, ALU.subtract, [psq, msq], [var])
                self.rstd(var[:], var[:], 1.0, [var], [var])
                for cc in range(2):
                    t1 = r_tmp.get()
                    self.tt("dve", t1[:], cs[cc][:], mean[:], ALU.subtract, [cs[cc], mean], [t1])
                    self.tt("dve", t1[:], t1[:], var[:], ALU.mult, [t1, var], [t1])
                    y = yb[cc].get()
                    self.act(y[:], t1[:], AF.Silu, [t1, self.vecs], [y], bias=self.vc(("a_lnb", l), cc),
                             scale=self.vc(("a_lng", l), cc))
                    ys.append((y, y[:]))
                gcols = [self.vc(("gbr", l), 0), self.vc(("gbr", l), 1)]
                row0 = 0
            else:
                for cc in range(2):
                    gb = r_gb.get()
                    self.P.dma("sp", gb[:], self.scr["gbC"][cc * 128:(cc + 1) * 128, t * TT:(t + 1) * TT],
                               [self.dbufs["gbC"][t]], [gb.b])
                    y = yb[cc].get()
                    self.tt("dve", y[:], cs[cc][:], gb[:], ALU.mult, [cs[cc], gb], [y])
                    ys.append((y, y[:]))
                gcols = [self.vc(("gbr", l), 4), self.vc(("gbr", l), 5)]
                row0 = 512
            self.group_norm_store(ys, gcols, row0, t, r_pst.get(), r_sq, r_rs.get(), r_ob)
        if scoped:
            self.pop()

    def head_norm_rope(self, ps, n, gcol, cf, sf, out_ap, out_T, R, W=None):
        TT = self.TT if W is None else W
        raw = R["raw"].get()
        self.cp("act", raw[0:n, :TT], ps[0:n, :TT], [ps], [raw])
        sq = R["sqh"].get()
        self.act(sq[0:n, :TT], ps[0:n, :TT], AF.Square, [ps], [sq])
        pss = R["pst"].get()
        self.mm(pss[0:n, :TT], self.ones[0:n, 0:n], sq[0:n, :TT], True, True, [self.ones, sq], [pss])
        rs = R["rsh"].get()
        self.rstd(rs[0:n, :TT], pss[0:n, :TT], n, [pss], [rs], 0, n)
        if cf is None:
            self.stt("dve", out_ap, raw[0:n, :TT], gcol, rs[0:n, :TT], ALU.mult, ALU.mult, [raw, rs, self.vecs], [out_T])
            return
        qn = R["qn"].get()
        self.stt("dve", qn[0:n, :], raw[0:n, :], gcol, rs[0:n, :], ALU.mult, ALU.mult, [raw, rs, self.vecs], [qn])
        qnb = R["qnb"].get()
        self.cp("pool", qnb[0:n, :], qn[0:n, :], [qn], [qnb])
        psr = R["pst"].get()
        self.mm(psr[0:n, :TT], self.rt[0:n, 0:n], qnb[0:n, :], True, True, [self.rt, qnb], [psr])
        t1 = R["raw"].get()
        self.tt("dve", t1[0:n, :], qn[0:n, :], cf[0:n, :], ALU.mult, [qn, cf], [t1])
        t2 = R["qn"].get()
        self.tt("dve", t2[0:n, :], psr[0:n, :TT], sf[0:n, :], ALU.mult, [psr, sf], [t2])
        self.tt("dve", out_ap, t1[0:n, :], t2[0:n, :], ALU.add, [t1, t2], [out_T])

    def hnr_multi(self, pss_in, n, gcol, cf, sf, outs, R, r_ps, W=None):
        TT = self.TT if W is None else W
        H = len(pss_in)
        raws, sqs, ps2, rss = [], [], [], []
        for h in range(H):
            raw = R["raw"].get()
            self.cp("act", raw[0:n, :TT], pss_in[h][0:n, :TT], [pss_in[h]], [raw])
            sq = R["sqh"].get()
            self.act(sq[0:n, :TT], pss_in[h][0:n, :TT], AF.Square, [pss_in[h]], [sq])
            raws.append(raw)
            sqs.append(sq)
        for h in range(H):
            p = r_ps.get()
            self.mm(p[0:n, :TT], self.ones[0:n, 0:n], sqs[h][0:n, :TT], True, True, [self.ones, sqs[h]], [p])
            ps2.append(p)
        for h in range(H):
            rs = R["rsh"].get()
            self.act(rs[0:n, :TT], ps2[h][0:n, :TT], AF.Ln, [ps2[h], self.vecs], [rs], bias=self.vc("eps", 0, 0, n), scale=1.0 / n)
            rss.append(rs)
        for h in range(H):
            self.act(rss[h][0:n, :TT], rss[h][0:n, :TT], AF.Exp, [rss[h]], [rss[h]], scale=-0.5)
        if cf is None:
            for h in range(H):
                self.stt("dve", outs[h][0], raws[h][0:n, :TT], gcol, rss[h][0:n, :TT], ALU.mult, ALU.mult,
                         [raws[h], rss[h], self.vecs], [outs[h][1]])
            return
        qns, qnbs, psrs, t1s, t2s = [], [], [], [], []
        for h in range(H):
            qn = R["qn"].get()
            self.stt("dve", qn[0:n, :], raws[h][0:n, :], gcol, rss[h][0:n, :], ALU.mult, ALU.mult, [raws[h], rss[h], self.vecs], [qn])
            qns.append(qn)
        for h in range(H):
            qnb = R["qnb"].get()
            self.cp("pool", qnb[0:n, :], qns[h][0:n, :], [qns[h]], [qnb])
            qnbs.append(qnb)
        for h in range(H):
            psr = r_ps.get()
            self.mm(psr[0:n, :TT], self.rt[0:n, 0:n], qnbs[h][0:n, :], True, True, [self.rt, qnbs[h]], [psr])
            psrs.append(psr)
        for h in range(H):
            t1 = R["raw"].get()
            self.tt("pool", t1[0:n, :], qns[h][0:n, :], cf[0:n, :], ALU.mult, [qns[h], cf], [t1])
            t1s.append(t1)
        for h in range(H):
            t2 = R["qn"].get()
            self.tt("dve", t2[0:n, :], psrs[h][0:n, :TT], sf[0:n, :], ALU.mult, [psrs[h], sf], [t2])
            t2s.append(t2)
        for h in range(H):
            self.tt("dve", outs[h][0], t1s[h][0:n, :], t2s[h][0:n, :], ALU.add, [t1s[h], t2s[h]], [outs[h][1]])

    def softmax_av_finish(self, pso, h, R, yD):
        TT = self.TT
        rd = R["rd"].get()
        self.P.op("dve", lambda e: e.reciprocal(rd[64:65, :], pso[64:65, :TT]), [pso.b], [rd.b])
        psb = R["pst"].get()
        self.mm(psb[0:64, :TT], self.ones_f[64:65, 0:64], rd[64:65, :], True, True, [self.ones_f, rd], [psb])
        osb = R["osb"].get()
        self.cp("act", osb[0:64, :], pso[0:64, :TT], [pso], [osb])
        self.tt("dve", yD[h][0:64, :], osb[0:64, :], psb[0:64, :TT], ALU.mult, [osb, psb], [yD[h]])

    def ph_mla(self, l):
        TT, NT, S = self.TT, self.NT, self.S
        NKB = S // 128
        self.push()
        wuq = self.sb("wuq", [128, 2, 384], BF16)
        self.P.dma("pool", wuq[:], self.wuq[l].rearrange("(c p) n -> p c n", p=128), [self.wbuf], [wuq.b])
        Wk = self.sb("Wk", [128, 4, 96], BF16)
        self.ms("dve", Wk[:], 0.0, [Wk])
        wkv4 = self.wukv[l].rearrange("p (h x) -> p h x", h=4)
        self.P.dma("pool", Wk[:, :, 0:64], wkv4[:, :, 0:64], [self.wbuf], [Wk.b])
        Wv = self.sb("Wv", [128, 4, 64], BF16)
        self.P.dma("pool", Wv[:], wkv4[:, :, 64:128], [self.wbuf], [Wv.b])
        KT = [self.sb(f"KT{h}", [96, S], BF16) for h in range(4)]
        Va = self.sb("Vaug", [128, NKB, 4, 65], BF16)
        self.ms("pool", Va[:, :, :, 64:65], 1.0, [Va])
        R = {"raw": self.rot("raw", [96, TT], F32, 8), "sqh": self.rot("sqh", [96, TT], BF16, 4),
             "pst": self.rot("pst", [128, 512], F32, 2, psum=True), "rsh": self.rot("rsh", [96, TT], F32, 4),
             "qn": self.rot("qn", [96, TT], F32, 8), "qnb": self.rot("qnb", [96, TT], BF16, 4),
             "rd": self.rot("rd", [65, TT], F32, 2), "osb": self.rot("osb", [64, TT], F32, 2)}
        r_ps = self.rot("ps", [128, 512], F32, 4, psum=True)
        r_pso = self.rot("pso", [128, 512], F32, 2, psum=True)
        r_cf = self.rot("cf", [96, TT], F32, 2)
        r_sf = self.rot("sf", [96, TT], F32, 2)
        r_in = self.rot("ckv", [128, TT], BF16, 2)
        r_kr = self.rot("kr", [32, TT], BF16, 2)
        r_sq = self.rot("sq", [128, TT], BF16, 2)
        r_rs = self.rot("rs", [128, TT], F32, 2)
        r_n = self.rot("ckvn", [128, TT], BF16, 2)
        r_cq = self.rot("cq", [128, 2, TT], BF16, 2)
        r_cqs = self.rot("cqs", [128, 2, TT], BF16, 1)
        r_cqn = self.rot("cqn", [128, 2, TT], BF16, 2)
        r_q = self.rot("Qh", [96, TT], BF16, 5)
        r_pe = self.rot("pe", [128, TT], BF16, 4)
        yD = [self.sb(f"yD{h}", [64, TT], F32) for h in range(4)]
        r_sqg = self.rot("sqg", [128, TT], BF16, 2)
        r_ob = self.rot("ob", [128, TT], BF16, 4)
        sl_t = lambda t: slice(t * TT, (t + 1) * TT)

        def load_cs(t):
            cf, sf = r_cf.get(), r_sf.get()
            self.P.dma("sp", cf[:], self.scr["Cf"][:, sl_t(t)], [self.dbufs["Cf"][t]], [cf.b])
            self.P.dma("sp", sf[:], self.scr["Sf"][:, sl_t(t)], [self.dbufs["Sf"][t]], [sf.b])
            return cf, sf

        for t in range(NT):
            ckv, kr = r_in.get(), r_kr.get()
            self.P.dma("sp", ckv[:], self.scr["ckv"][:, sl_t(t)], [self.dbufs["ckv"][t]], [ckv.b])
            self.P.dma("sp", kr[:], self.scr["kr"][:, sl_t(t)], [self.dbufs["kr"][t]], [kr.b])
            cf, sf = load_cs(t)
            sq = r_sq.get()
            self.act(sq[:], ckv[:], AF.Square, [ckv], [sq])
            pss = R["pst"].get()
            self.mm(pss[:, :TT], self.ones[:], sq[:], True, True, [self.ones, sq], [pss])
            rs = r_rs.get()
            self.rstd(rs[:], pss[:, :TT], 128, [pss], [rs])
            cn = r_n.get()
            self.stt("dve", cn[:], ckv[:], self.vc(("kva_g", l)), rs[:], ALU.mult, ALU.mult, [ckv, rs, self.vecs], [cn])
            psks = []
            for h in range(4):
                psk = r_ps.get()
                self.mm(psk[0:96, :TT], Wk[:, h, :], cn[:], True, False, [Wk, cn], [psk])
                self.mm(psk[0:96, :TT], self.sh[0:32, 0:96], kr[0:32, :], False, True, [self.sh, kr], [psk])
                psks.append(psk)
            self.hnr_multi(psks, 96, self.vc(("kn_g", l), 0, 0, 96), cf, sf,
                           [(KT[h][0:96, sl_t(t)], KT[h]) for h in range(4)], R, r_ps)
            for tb in range(TT // 128):
                psv = r_ps.get()
                self.mm(psv[:, 0:256], cn[:, tb * 128:(tb + 1) * 128], Wv[:].rearrange("p h e -> p (h e)"), True, True,
                        [cn, Wv], [psv])
                kb = t * (TT // 128) + tb
                self.cp("act", Va[:, kb, :, 0:64], psv[:, 0:256].rearrange("p (h e) -> p h e", h=4), [psv], [Va])
        sc = 96.0 ** -0.5
        for t in range(NT):
            cq = r_cq.get()
            self.P.dma("sp", cq[:], self.scr["cq"].rearrange("(c p) s -> p c s", p=128)[:, :, sl_t(t)],
                       [self.dbufs["cq"][t]], [cq.b])
            cf, sf = load_cs(t)
            cqs = r_cqs.get()
            self.act(cqs[:], cq[:], AF.Square, [cq], [cqs])
            pss = R["pst"].get()
            for c in range(2):
                self.mm(pss[:, :TT], self.ones[:], cqs[:, c, :], c == 0, c == 1, [self.ones, cqs], [pss])
            rs = r_rs.get()
            self.rstd(rs[:], pss[:, :TT], 256, [pss], [rs])
            cqn = r_cqn.get()
            for c in range(2):
                self.stt("dve", cqn[:, c, :], cq[:, c, :], self.vc(("qa_g", l), c), rs[:], ALU.mult, ALU.mult,
                         [cq, rs, self.vecs], [cqn])
            Qs, psqs = [], []
            for h in range(4):
                psq = r_ps.get()
                for c in range(2):
                    self.mm(psq[0:96, :TT], wuq[:, c, h * 96:(h + 1) * 96], cqn[:, c, :], c == 0, c == 1, [wuq, cqn], [psq])
                psqs.append(psq)
                Qs.append(r_q.get())
            self.hnr_multi(psqs, 96, self.vc(("qn_g", l), 0, 0, 96), cf, sf, [(Qs[h][0:96, :], Qs[h]) for h in range(4)], R, r_ps)
            for h in range(4):
                Qh = Qs[h]
                pso = r_pso.get()

                def scores(kb, h=h, Qh=Qh):
                    p = r_ps.get()
                    self.mm(p[:, :TT], KT[h][0:96, kb * 128:(kb + 1) * 128], Qh[0:96, :], True, True, [KT[h], Qh], [p])
                    return p

                LA = 2
                q_sc = [scores(kb) for kb in range(min(LA, NKB))]
                for kb in range(NKB):
                    pssc = q_sc.pop(0)
                    if kb + LA < NKB:
                        q_sc.append(scores(kb + LA))
                    pe = r_pe.get()
                    self.act(pe[:], pssc[:, :TT], AF.Exp, [pssc], [pe], scale=sc)
                    self.mm(pso[0:65, :TT], Va[:, kb, h, :], pe[:], kb == 0, kb == NKB - 1, [Va, pe], [pso])
                self.softmax_av_finish(pso, h, R, yD)
            self.group_norm_store([(yD[h], yD[h][0:64, :]) for h in range(4)],
                                  [self.vc(("gbrD", l), h, 0, 64) for h in range(4)], 768, t,
                                  R["pst"].get(), r_sqg, r_rs.get(), r_ob, npart=64)
        self.pop()

    def ph_hgrn(self, l):
        TT, NT, S = self.TT, self.NT, self.S
        NB = TT // 128
        NCB = 128 // CH
        NCT = TT // CH
        self.push()
        oacc = [self.sb(f"oacc{hp}", [128, S], F32) for hp in range(2)]
        oaccb = [self.sb(f"oaccb{hp}", [128, S], F32) for hp in range(2)]
        r_sst = [self.rot(f"Sst{hp}", [128, 128], F32, 4) for hp in range(2)]
        r_sbf = [self.rot(f"Sbf{hp}", [128, 128], BF16, 4) for hp in range(2)]
        PADW = 16
        PW = CH + PADW
        v3 = lambda ap: ap.rearrange("p (c j) -> p c j", j=CH)
        r_lf = self.rot("lf", [128, NCT, PW], F32, 2)
        for tl in r_lf.tiles:
            self.ms("dve", tl[:, :, 0:PADW], 0.0, [tl])
        cs_a = self.sb("cs_a", [128, NCT, PW], F32)
        cs_b = self.sb("cs_b", [128, NCT, PW], F32)
        self.ms("dve", cs_a[:, :, 0:PADW], 0.0, [cs_a])
        self.ms("dve", cs_b[:, :, 0:PADW], 0.0, [cs_b])

        def shadd(dst_ap, src, sh, dst_T):
            self.tt("pool", dst_ap, src[:, :, PADW:PW], src[:, :, PADW - sh:PW - sh], ALU.add, [src], [dst_T])

        r_q = self.rot("q", [128, TT], BF16, 2)
        r_b = self.rot("b", [128, TT], F32, 3)
        r_f = self.rot("f32t", [128, TT], F32, 6)
        r_qt = self.rot("qt", [128, TT], BF16, 5)
        r_kt = self.rot("kt", [128, TT], BF16, 9)
        r_kh = self.rot("kh", [128, TT], BF16, 5)
        r_eb = self.rot("eb", [128, TT], F32, 5)
        r_tp = self.rot("tp", [128, NCB, 128], BF16, 1, psum=True)
        r_pat = self.rot("pat", [128, 512], F32, 1, psum=True)
        r_pio = self.rot("pio", [128, 512], F32, 3, psum=True)
        r_pu = self.rot("pu4", [128, 512], F32, 3, psum=True)
        r_kht = self.rot("kht", [32, NCB, 128], BF16, 3)
        r_vjt = self.rot("vjt", [32, NCT, 128], BF16, 5)
        r_vbt = self.rot("vbt", [128, NB, 128], BF16, 5)
        r_atm = self.rot("atm", [128, 256], BF16, 2)
        r_isb = self.rot("isb", [128, 128], F32, 2)
        r_tmp = self.rot("ctmp", [128, 128], F32, 2)
        sl_t = lambda t: slice(t * TT, (t + 1) * TT)
        for d in range(2):
            fwd = d == 0
            mask = self.mf if fwd else self.mb
            di = CH - 1 if fwd else 0
            cur = []
            curS = []
            for hp in range(2):
                z0 = r_sst[hp].get()
                self.ms("dve", z0[:], 0.0, [z0])
                curS.append(z0)
                z = r_sbf[hp].get()
                self.ms("pool", z[:], 0.0, [z])
                cur.append(z)
            prepd = {}

            def prep(t, d=d, fwd=fwd, di=di, prepd=prepd):
                for hp in range(2):
                    rows = slice(hp * 128, (hp + 1) * 128)
                    lf, q = r_lf.get(), r_q.get()
                    lfv = lf[:, :, PADW:PW]
                    self.P.dma("sp", lfv, v3(self.scr["lf"][d, rows, sl_t(t)]), [self.dbufs["lf"][t]], [lf.b])
                    self.P.dma("sp", q[:], self.scr["qh"][rows, sl_t(t)], [self.dbufs["qh"][t]], [q.b])
                    b = r_b.get()
                    shadd(cs_a[:, :, PADW:PW], lf, 1, cs_a)
                    shadd(cs_b[:, :, PADW:PW], cs_a, 2, cs_b)
                    shadd(cs_a[:, :, PADW:PW], cs_b, 4, cs_a)
                    shadd(cs_b[:, :, PADW:PW], cs_a, 8, cs_b)
                    shadd(v3(b[:]), cs_b, 16, b)
                    if not fwd:
                        b3 = v3(b[:])
                        tmp = r_f.get()
                        self.tt("pool", v3(tmp[:]), b3[:, :, CH - 1:CH].to_broadcast([128, NCT, CH]), b3, ALU.subtract, [b], [tmp])
                        bb = r_b.get()
                        self.tt("pool", v3(bb[:]), v3(tmp[:]), lfv, ALU.add, [tmp, lf], [bb])
                        b = bb
                    eb, enb, ef = r_eb.get(), r_f.get(), r_f.get()
                    self.act(eb[:], b[:], AF.Exp, [b], [eb])
                    self.act(enb[:], b[:], AF.Exp, [b], [enb], scale=-1.0)
                    self.act(v3(ef[:]), lfv, AF.Exp, [lf], [ef])
                    k = r_f.get()
                    self.ts("dve", k[:], ef[:], -1.0, 1.0, ALU.mult, ALU.add, [ef], [k])
                    qt = r_qt.get()
                    self.tt("dve", qt[:], q[:], eb[:], ALU.mult, [q, eb], [qt])
                    kt32 = r_f.get()
                    self.tt("dve", kt32[:], k[:], enb[:], ALU.mult, [k, enb], [kt32])
                    kt = []
                    for h2 in range(2):
                        km = r_kt.get()
                        self.ts("dve", km[:], kt32[:], self.vc("hm%d" % h2), None, ALU.mult, None, [kt32, self.vecs], [km])
                        kt.append(km)
                    eb3 = v3(eb[:])
                    kh = r_kh.get()
                    self.tt("dve", v3(kh[:]), v3(kt32[:]), eb3[:, :, di:di + 1].to_broadcast([128, NCT, CH]), ALU.mult, [kt32, eb], [kh])
                    cols = slice(hp * 128, (hp + 1) * 128)
                    vjt, vbt = r_vjt.get(), r_vbt.get()
                    vsrc = self.scr["vh"][t * TT:(t + 1) * TT, cols]
                    self.P.dma("sp", vjt[:], vsrc.rearrange("(c j) e -> j c e", j=CH), [self.dbufs["vh"][t]], [vjt.b])
                    self.P.dma("sp", vbt[:], vsrc.rearrange("(b p) e -> p b e", p=128), [self.dbufs["vh"][t]], [vbt.b])
                    prepd[(t, hp)] = (qt, kt, kh, eb, vjt, vbt)

            def stage_a(u, mask=mask, prepd=prepd):
                t, bi, hp = u
                qt, kt, kh, eb, vjt, vbt = prepd[(t, hp)]
                g0 = t * TT + bi * 128
                bc = slice(bi * 128, (bi + 1) * 128)
                cols = slice(hp * 128, (hp + 1) * 128)
                tp = r_tp.get()
                for c in range(NCB):
                    self.tr(tp[0:CH, c, :], kh[:, bi * 128 + c * CH:bi * 128 + (c + 1) * CH], self.ident[:], [kh, self.ident], [tp])
                kht = r_kht.get()
                self.cp("act", kht[:], tp[0:CH, :, :], [tp], [kht])
                pat = r_pat.get()
                for h2 in range(2):
                    self.mm(pat[:, h2 * 128:(h2 + 1) * 128], kt[h2][:, bc], qt[:, bc], True, True, [kt[h2], qt], [pat])
                atm = r_atm.get()
                self.tt("dve", atm[:], pat[:, 0:256], mask[:], ALU.mult, [pat, mask], [atm])
                pio = r_pio.get()
                self.mm(pio[:, 0:256], vbt[:, bi, :], atm[:], True, True, [vbt, atm], [pio])
                pu = r_pu.get()
                for c in range(NCB):
                    self.mm(pu[:, c * 128:(c + 1) * 128], kht[0:CH, c, :], vjt[0:CH, bi * NCB + c, :], True, True, [kht, vjt], [pu])
                return pio, pu

            def stage_b(u, pio, pu, fwd=fwd, di=di, cur=cur, curS=curS, prepd=prepd):
                t, bi, hp = u
                qt, kt, kh, eb, vjt, vbt = prepd[(t, hp)]
                g0 = t * TT + bi * 128
                for c in (range(NCB) if fwd else range(NCB - 1, -1, -1)):
                    cc = slice(bi * 128 + c * CH, bi * 128 + (c + 1) * CH)
                    self.mm(pio[:, 256 + c * CH:256 + (c + 1) * CH], cur[hp][:], qt[:, cc], True, True, [cur[hp], qt], [pio])
                    dcol = bi * 128 + c * CH + di
                    sn = r_sst[hp].get()
                    self.stt("dve", sn[:], curS[hp][:], eb[:, dcol:dcol + 1], pu[:, c * 128:(c + 1) * 128], ALU.mult, ALU.add,
                             [curS[hp], eb, pu], [sn])
                    curS[hp] = sn
                    nxt = r_sbf[hp].get()
                    self.tt("pool", nxt[:], sn[:], self.bd_f[:], ALU.mult, [sn, self.bd_f], [nxt])
                    cur[hp] = nxt
                isb = r_isb.get()
                self.cp("act", isb[:], pio[:, 256:384], [pio], [isb])
                gs = slice(g0, g0 + 128)
                if fwd:
                    self.tt("dve", oacc[hp][0:64, gs], pio[0:64, 0:128], isb[0:64, :], ALU.add, [pio, isb], [oacc[hp]])
                    self.tt("dve", oacc[hp][64:128, gs], pio[64:128, 128:256], isb[64:128, :], ALU.add, [pio, isb], [oacc[hp]])
                else:
                    self.tt("dve", oaccb[hp][0:64, gs], pio[0:64, 0:128], isb[0:64, :], ALU.add, [pio, isb], [oaccb[hp]])
                    self.tt("dve", oaccb[hp][64:128, gs], pio[64:128, 128:256], isb[64:128, :], ALU.add, [pio, isb], [oaccb[hp]])

            units = []
            for t in (range(NT) if fwd else range(NT - 1, -1, -1)):
                for bi in (range(NB) if fwd else range(NB - 1, -1, -1)):
                    for hp in range(2):
                        units.append((t, bi, hp))
            prep(units[0][0])
            pend = stage_a(units[0])
            for i, u in enumerate(units):
                nxt_pend = None
                if i + 1 < len(units):
                    if units[i + 1][0] != u[0]:
                        prep(units[i + 1][0])
                    nxt_pend = stage_a(units[i + 1])
                stage_b(u, *pend)
                pend = nxt_pend
        r_sq = self.rot("sqb", [128, TT], BF16, 2)
        r_rs = self.rot("rs", [128, TT], F32, 2)
        r_sg = self.rot("sgt", [128, TT], BF16, 2)
        yb = [self.rot(f"yb{hp}", [128, TT], F32, 2) for hp in range(2)]
        r_ob = self.rot("ob", [128, TT], BF16, 4)
        for t in range(NT):
            if HG_STAGE < 6:
                break
            ys = []
            for hp in range(2):
                o = oacc[hp][:, sl_t(t)]
                self.tt("pool", o, o, oaccb[hp][:, sl_t(t)], ALU.add, [oacc[hp], oaccb[hp]], [oacc[hp]])
                sq = r_sq.get()
                self.act(sq[:], o, AF.Square, [oacc[hp]], [sq])
                pss = r_pat.get()
                self.mm(pss[:, :TT], self.bd[:], sq[:], True, True, [self.bd, sq], [pss])
                rs = r_rs.get()
                self.rstd(rs[:], pss[:, :TT], 64, [pss], [rs])
                sg = r_sg.get()
                self.P.dma("sp", sg[:], self.scr["sg"][hp * 128:(hp + 1) * 128, sl_t(t)], [self.dbufs["sg"][t]], [sg.b])
                y = yb[hp].get()
                self.stt("dve", y[:], o, self.vc(("onorm", l), hp), rs[:], ALU.mult, ALU.mult, [oacc[hp], rs, self.vecs], [y])
                self.tt("dve", y[:], y[:], sg[:], ALU.mult, [y, sg], [y])
                ys.append((y, y[:]))
            self.group_norm_store(ys, [self.vc(("gbr", l), 2), self.vc(("gbr", l), 3)], 256, t, r_pat.get(), r_sq,
                                  r_rs.get(), r_ob)
        self.pop()

    def store_x(self, xt, t):
        TT = self.TT
        dst = self.y.rearrange("(c p) s -> p c s", p=128)[:, :, t * TT:(t + 1) * TT]
        self.P.dma("pool", dst, xt[:], [xt.b], [self.dbufs["y"][t]])

    def ph_outproj(self, l, xsrc, xname):
        TT, NT = self.TT, self.NT
        self.push()
        w = [self.sb(f"w_out{kc}", [128, D], BF16) for kc in range(KC)]
        for kc in range(KC):
            self.P.dma("pool", w[kc][:], self.w_out[l, kc * 128:(kc + 1) * 128, :], [self.wbuf], [w[kc].b])
        r_xt = self.rot("xt", [128, KC, TT], F32, 2)
        r_cat = self.rot("cat", [128, KC, TT], BF16, 2)
        r_ps = self.rot("ps", [128, 512], F32, 4, psum=True)
        for t in range(NT):
            xt, cat = r_xt.get(), r_cat.get()
            self.load_x(xsrc, xname, t, xt)
            self.P.dma("sp", cat[:], self.scr["cat"].rearrange("(c p) s -> p c s", p=128)[:, :, t * TT:(t + 1) * TT],
                       [self.dbufs["cat"][t]], [cat.b])
            for m in range(KC):
                ps = r_ps.get()
                for kc in range(KC):
                    self.mm(ps[:, :TT], w[kc][:, m * 128:(m + 1) * 128], cat[:, kc, :], kc == 0, kc == KC - 1, [w[kc], cat], [ps])
                self.tt("dve", xt[:, m, :], xt[:, m, :], ps[:, :TT], ALU.add, [xt, ps], [xt])
            self.store_x(xt, t)
        self.pop()

    def ph_xattn(self, l, xsrc=None, xname=None):
        TT, NT = self.TT, self.NT
        M = NMEM
        self.push()
        fuse = xsrc is not None
        if fuse:
            wout = [self.sb(f"w_out{kc}", [128, D], BF16) for kc in range(KC)]
            for kc in range(KC):
                self.P.dma("pool", wout[kc][:], self.w_out[l, kc * 128:(kc + 1) * 128, :], [self.wbuf], [wout[kc].b])
            r_cat = self.rot("cat", [128, KC, TT], BF16, 2)
        wq = [self.sb(f"xwq{kc}", [128, 256], BF16) for kc in range(KC)]
        wk = self.sb("xwk", [128, KC, 4, 64], BF16)
        wv = self.sb("xwv", [128, KC, 4, 64], BF16)
        wo = self.sb("xwo", [64, 4, D], BF16)
        kv5 = self.x_wkv[l].rearrange("(c p) (h x) -> p c h x", p=128, h=4)
        for kc in range(KC):
            self.P.dma("pool", wq[kc][:], self.x_wq[l, kc * 128:(kc + 1) * 128, :], [self.wbuf], [wq[kc].b])
        for kc in range(KC):
            self.P.dma("pool", wk[:, kc, :, :], kv5[:, kc, :, 0:64], [self.wbuf], [wk.b])
            self.P.dma("pool", wv[:, kc, :, :], kv5[:, kc, :, 64:128], [self.wbuf], [wv.b])
        self.P.dma("pool", wo[:], self.x_wo[l].rearrange("(h p) n -> p h n", p=64), [self.wbuf], [wo.b])
        R = {"raw": self.rot("raw", [64, TT], F32, 4), "sqh": self.rot("sqh", [64, TT], BF16, 4),
             "pst": self.rot("pst", [128, 512], F32, 2, psum=True), "rsh": self.rot("rsh", [64, TT], F32, 4),
             "rd": self.rot("rd", [65, TT], F32, 2), "osb": self.rot("osb", [64, TT], F32, 2)}
        r_ps = self.rot("ps", [128, 512], F32, 4, psum=True)
        r_pso = self.rot("pso", [128, 512], F32, 2, psum=True)
        mt = self.sb("memT", [128, KC, M], F32)
        self.P.dma("sp", mt[:], self.memT.rearrange("(c p) m -> p c m", p=128), [], [mt.b])
        msq = self.sb("msq", [128, KC, M], BF16)
        self.act(msq[:], mt[:], AF.Square, [mt], [msq])
        pss = R["pst"].get()
        for c in range(KC):
            self.mm(pss[:, :M], self.ones[:], msq[:, c, :], c == 0, c == KC - 1, [self.ones, msq], [pss])
        mrs = self.sb("mrs", [128, M], F32)
        self.rstd(mrs[:], pss[:, :M], D, [pss], [mrs])
        mn = self.sb("mn", [128, KC, M], BF16)
        for c in range(KC):
            self.stt("dve", mn[:, c, :], mt[:, c, :], self.vc(("g_mem", l), c), mrs[:], ALU.mult, ALU.mult, [mt, mrs, self.vecs], [mn])
        KmT = [self.sb(f"KmT{h}", [64, M], BF16) for h in range(4)]
        Vm = self.sb("Vm", [128, 2, 4, 65], BF16)
        self.ms("pool", Vm[:, :, :, 64:65], 1.0, [Vm])
        for h in range(4):
            psk = r_ps.get()
            for c in range(KC):
                self.mm(psk[0:64, :M], wk[:, c, h, :], mn[:, c, :], c == 0, c == KC - 1, [wk, mn], [psk])
            self.head_norm_rope(psk, 64, self.vc(("xkn", l), 0, 0, 64), None, None, KmT[h][0:64, :], KmT[h], R, W=M)
        for blk in range(2):
            psv = r_ps.get()
            for c in range(KC):
                self.mm(psv[:, 0:256], mn[:, c, blk * 128:(blk + 1) * 128], wv[:, c, :, :].rearrange("p h e -> p (h e)"),
                        c == 0, c == KC - 1, [wv, mn], [psv])
            self.cp("act", Vm[:, blk, :, 0:64], psv[:, 0:256].rearrange("p (h e) -> p h e", h=4), [psv], [Vm])
        r_xt = self.rot("xt", [128, KC, TT], F32, 2)
        r_sq = self.rot("sq", [128, KC, TT], BF16, 1)
        r_rs = self.rot("rs", [128, TT], F32, 2)
        r_nT = self.rot("nT", [128, KC, TT], BF16, 1)
        r_q = self.rot("Qx", [64, TT], BF16, 5)
        r_pe = self.rot("pe", [128, TT], BF16, 3)
        yX = [self.sb(f"yX{h}", [64, TT], F32) for h in range(4)]
        oX = [self.sb(f"oX{h}", [64, TT], BF16) for h in range(4)]
        sc = 64.0 ** -0.5
        for t in range(NT):
            xt = r_xt.get()
            if fuse:
                cat = r_cat.get()
                self.load_x(xsrc, xname, t, xt)
                self.P.dma("sp", cat[:], self.scr["cat"].rearrange("(c p) s -> p c s", p=128)[:, :, t * TT:(t + 1) * TT],
                           [self.dbufs["cat"][t]], [cat.b])
                for m in range(KC):
                    ps = r_ps.get()
                    for kc in range(KC):
                        self.mm(ps[:, :TT], wout[kc][:, m * 128:(m + 1) * 128], cat[:, kc, :], kc == 0, kc == KC - 1, [wout[kc], cat], [ps])
                    self.tt("dve", xt[:, m, :], xt[:, m, :], ps[:, :TT], ALU.add, [xt, ps], [xt])
            else:
                self.load_x(self.y, "y", t, xt)
            sq, rs, nT = r_sq.get(), r_rs.get(), r_nT.get()
            self.norm_tile(xt, "g_xq", l, sq, rs, nT, R["pst"].get())
            Qs, psqs = [], []
            for h in range(4):
                psq = r_ps.get()
                for c in range(KC):
                    self.mm(psq[0:64, :TT], wq[c][:, h * 64:(h + 1) * 64], nT[:, c, :], c == 0, c == KC - 1, [wq[c], nT], [psq])
                psqs.append(psq)
                Qs.append(r_q.get())
            self.hnr_multi(psqs, 64, self.vc(("xqn", l), 0, 0, 64), None, None, [(Qs[h][0:64, :], Qs[h]) for h in range(4)], R, r_ps)
            for h in range(4):
                Qx = Qs[h]
                pso = r_pso.get()
                pscs = []
                for blk in range(2):
                    pssc = r_ps.get()
                    self.mm(pssc[:, :TT], KmT[h][0:64, blk * 128:(blk + 1) * 128], Qx[0:64, :], True, True, [KmT[h], Qx], [pssc])
                    pscs.append(pssc)
                for blk in range(2):
                    pssc = pscs[blk]
                    pe = r_pe.get()
                    self.act(pe[:], pssc[:, :TT], AF.Exp, [pssc], [pe], scale=sc)
                    self.mm(pso[0:65, :TT], Vm[:, blk, h, :], pe[:], blk == 0, blk == 1, [Vm, pe], [pso])
                self.softmax_av_finish(pso, h, R, yX)
                self.cp("pool", oX[h][:], yX[h][:], [yX[h]], [oX[h]])
            for m in range(KC):
                ps = r_ps.get()
                for h in range(4):
                    self.mm(ps[:, :TT], wo[0:64, h, m * 128:(m + 1) * 128], oX[h][0:64, :], h == 0, h == 3, [wo, oX[h]], [ps])
                self.tt("dve", xt[:, m, :], xt[:, m, :], ps[:, :TT], ALU.add, [xt, ps], [xt])
            self.store_x(xt, t)
        self.pop()

    def ph_ffn(self, l):
        TT, NT = self.TT, self.NT
        self.push()
        JG = [(0, 6), (6, 12), (12, 17), (17, 22)]
        jgrp = {}
        for gi, (j0, j1) in enumerate(JG):
            for j in range(j0, j1):
                jgrp[j] = (gi, j - j0)
        w13 = [[self.sb(f"w13_{kc}_{gi}", [128, 2, (j1 - j0) * 128], BF16) for gi, (j0, j1) in enumerate(JG)] for kc in range(KC)]
        for gi, (j0, j1) in enumerate(JG):
            for kc in range(KC):
                src = self.f_w13[l, kc * 128:(kc + 1) * 128, :].rearrange("p (two n) -> p two n", two=2)[:, :, j0 * 128:j1 * 128]
                self.P.dma("pool", w13[kc][gi][:], src, [self.wbuf], [w13[kc][gi].b])
        w2v = self.f_w2[l].rearrange("(j p) n -> p j n", p=128)
        w2 = [self.sb(f"w2_{j}", [128, D], BF16) for j in range(NJ)]
        for j in range(NJ):
            self.P.dma("pool", w2[j][:], w2v[:, j, :], [self.wbuf], [w2[j].b])
        r_xt = self.rot("xt", [128, KC, TT], F32, 2)
        r_rs = self.rot("rs", [128, TT], F32, 1)
        r_nT = self.rot("nT", [128, KC, TT], BF16, 1)
        hid = self.sb("hid", [128, NJ, TT], BF16)
        r_s = self.rot("silu", [128, TT], F32, 2)
        r_ps = self.rot("ps", [128, 512], F32, 7, psum=True)
        r_pst = self.rot("pst", [128, 512], F32, 1, psum=True)
        for t in range(NT):
            xt = r_xt.get()
            self.load_x(self.y, "y", t, xt)
            rs, nT = r_rs.get(), r_nT.get()
            self.act(hid[:, 0:KC, :], xt[:], AF.Square, [xt], [hid])
            pst = r_pst.get()
            for c in range(KC):
                self.mm(pst[:, :TT], self.ones[:], hid[:, c, :], c == 0, c == KC - 1, [hid, self.ones], [pst])
            self.rstd(rs[:], pst[:, :TT], D, [pst], [rs])
            for c in range(KC):
                self.stt("dve", nT[:, c, :], xt[:, c, :], self.vc(("g_ffn", l), c), rs[:], ALU.mult, ALU.mult,
                         [xt, rs, self.vecs], [nT])
            for j in range(NJ):
                p1, p3 = r_ps.get(), r_ps.get()
                gi, jj = jgrp[j]
                for kc in range(KC):
                    self.mm(p1[:, :TT], w13[kc][gi][:, 0, jj * 128:(jj + 1) * 128], nT[:, kc, :], kc == 0, kc == KC - 1,
                            [w13[kc][gi], nT], [p1])
                for kc in range(KC):
                    self.mm(p3[:, :TT], w13[kc][gi][:, 1, jj * 128:(jj + 1) * 128], nT[:, kc, :], kc == 0, kc == KC - 1,
                            [w13[kc][gi], nT], [p3])
                s = r_s.get()
                self.act(s[:], p1[:, :TT], AF.Silu, [p1], [s])
                self.tt("dve", hid[:, j, :], s[:], p3[:, :TT], ALU.mult, [s, p3], [hid])
            for m in range(KC):
                ps = r_ps.get()
                for j in range(NJ):
                    self.mm(ps[:, :TT], w2[j][:, m * 128:(m + 1) * 128], hid[:, j, :], j == 0, j == NJ - 1,
                            [w2[j], hid], [ps])
                self.tt("dve", xt[:, m, :], xt[:, m, :], ps[:, :TT], ALU.add, [xt, ps], [xt])
            self.store_x(xt, t)
        self.pop()

    def build(self, phases="all"):
        on = lambda p: phases == "all" or p in phases.split(",")
        self.setup()
        for l in range(self.L):
            xsrc, xname = (self.xin, "xin") if l == 0 else (self.y, "y")
            if on("inproj"):
                self.ph_inproj(l, xsrc, xname)
            if on("convA") and on("convC"):
                self.push()
                self.ph_conv(l, "A", scoped=False)
                self.ph_conv(l, "C", scoped=False)
                self.pop()
            else:
                if on("convA"):
                    self.ph_conv(l, "A")
                if on("convC"):
                    self.ph_conv(l, "C")
            if on("hgrn"):
                self.ph_hgrn(l)
            if on("mla"):
                self.ph_mla(l)
            if on("outproj") and on("xattn"):
                self.ph_xattn(l, xsrc, xname)
            else:
                if on("outproj"):
                    self.ph_outproj(l, xsrc, xname)
                if on("xattn"):
                    self.ph_xattn(l)
            if on("ffn"):
                self.ph_ffn(l)
        self.P.finish_wait_all(self.dbufs["y"])
        self.pop()
        self.P.close()
        return self.nc


def host_inputs(inp, L, S):
    B = inp["x"].shape[0]
    cm = const_mats()
    vecs = pack_vecs(inp, L)
    shared = {"vecs": vecs}
    for k, v in cm.items():
        shared["c_" + k] = v
    for k in ["w_in", "m_wuq", "m_wukv", "w_out", "x_wq", "x_wkv", "x_wo", "f_w13", "f_w2"]:
        shared[k] = np.ascontiguousarray(np.asarray(inp[k], np.float32)[:L])
    maps = []
    for b in range(B):
        m = dict(shared)
        m["xT"] = np.ascontiguousarray(np.asarray(inp["x"][b], np.float32).T)
        m["memT"] = np.ascontiguousarray(np.asarray(inp["mem"][b], np.float32).T)
        m["pos"] = np.ascontiguousarray(np.asarray(inp["positions"][b], np.int32).reshape(1, S))
        maps.append(m)
    return maps


def kernel(**inputs):
    L = inputs["w_in"].shape[0]
    B, S, _ = inputs["x"].shape
    kb = KB(S, L, TT=512)
    nc = kb.build()
    maps = host_inputs(inputs, L, S)
    res = run_bass_kernel_spmd(nc, maps, core_ids=list(range(B)))
    out = np.stack([np.asarray(r["y"], np.float32).T for r in res.results], axis=0)
    return np.ascontiguousarray(out)
```

```python
import os
import numpy as np
import ml_dtypes
import concourse.bass as bass
import concourse.mybir as mybir
from concourse.bass_utils import run_bass_kernel_spmd

F32 = mybir.dt.float32
BF16 = mybir.dt.bfloat16
I32 = mybir.dt.int32
AF = mybir.ActivationFunctionType
ALU = mybir.AluOpType
AX = mybir.AxisListType


class Buf:
    __slots__ = ("name", "w", "r")

    def __init__(self, name):
        self.name = name
        self.w = {}
        self.r = {}


class Prog:
    NSLOT = {"sp": 8, "pool": 6, "act": 2}

    def __init__(self, nc):
        self.nc = nc
        self.ops = {"pe": [], "act": [], "dve": [], "pool": [], "sp": []}
        self.count = {k: 0 for k in self.ops}
        self.waited = {k: {} for k in self.ops}
        self.sems = {}
        self.dma_sems = {}
        self.dma_uses = {}
        self.dma_next = {}
        self._ctx = []

    def enter(self, cm):
        v = cm.__enter__()
        self._ctx.append(cm)
        return v

    def setup_sems(self):
        for k in self.ops:
            self.sems[k] = self.enter(self.nc.semaphore("s_" + k))
        for q, n in self.NSLOT.items():
            self.dma_sems[q] = [self.enter(self.nc.semaphore(f"d_{q}{i}")) for i in range(n)]
            self.dma_uses[q] = [0] * n
            self.dma_next[q] = 0

    def _deps(self, eng, reads, writes):
        deps = {}
        for b in reads:
            for sid, (s, v) in b.w.items():
                if sid not in deps or deps[sid][1] < v:
                    deps[sid] = (s, v)
        for b in writes:
            for d in (b.w, b.r):
                for sid, (s, v) in d.items():
                    if sid not in deps or deps[sid][1] < v:
                        deps[sid] = (s, v)
        waits = []
        wd = self.waited[eng]
        own = id(self.sems[eng])
        for sid, (s, v) in deps.items():
            if eng == "pe" and sid == own:
                continue
            if wd.get(sid, 0) >= v:
                continue
            wd[sid] = v
            waits.append((s, v))
        return waits

    def _commit(self, tok, reads, writes):
        sid = id(tok[0])
        for b in reads:
            b.r[sid] = tok
        for b in writes:
            b.w = {sid: tok}
            b.r = {}

    def op(self, eng, fn, reads=(), writes=()):
        waits = self._deps(eng, reads, writes)
        self.count[eng] += 1
        tok = (self.sems[eng], self.count[eng])
        self.ops[eng].append((waits, fn, (tok[0], 1)))
        self._commit(tok, reads, writes)

    def dma(self, q, out_ap, in_ap, reads=(), writes=(), **kw):
        eng = {"sp": "sp", "pool": "pool", "act": "act"}[q]
        slot = self.dma_next[q]
        self.dma_next[q] = (slot + 1) % len(self.dma_sems[q])
        sem = self.dma_sems[q][slot]
        waits = self._deps(eng, reads, writes)
        prev = self.dma_uses[q][slot] * 16
        if prev and self.waited[eng].get(id(sem), 0) < prev:
            self.waited[eng][id(sem)] = prev
            waits.append((sem, prev))
        self.dma_uses[q][slot] += 1
        tok = (sem, self.dma_uses[q][slot] * 16)
        self.ops[eng].append((waits, lambda e: e.dma_start(out=out_ap, in_=in_ap, **kw), (sem, 16)))
        self._commit(tok, reads, writes)

    def barrier(self):
        toks = [(self.sems[k], self.count[k]) for k in self.ops if self.count[k] > 0]
        for q, sems in self.dma_sems.items():
            for i, s in enumerate(sems):
                if self.dma_uses[q][i] > 0:
                    toks.append((s, self.dma_uses[q][i] * 16))
        for eng in self.ops:
            waits = []
            wd = self.waited[eng]
            for s, v in toks:
                if wd.get(id(s), 0) >= v:
                    continue
                wd[id(s)] = v
                waits.append((s, v))
            if waits:
                self.ops[eng].append((waits, None, None))

    def finish_wait_all(self, bufs):
        waits = self._deps("sp", bufs, [])
        self.ops["sp"].append((waits, None, None))

    def emit(self):
        nc = self.nc
        with nc.Block() as block:
            def run(eng_name):
                def f(e):
                    for waits, fn, inc in self.ops[eng_name]:
                        for s, v in waits:
                            e.wait_ge(s, v)
                        if fn is not None:
                            ins = fn(e)
                            if inc is not None:
                                ins.then_inc(inc[0], inc[1])
                return f
            block.tensor(run("pe"))
            block.scalar(run("act"))
            block.vector(run("dve"))
            block.gpsimd(run("pool"))
            block.sync(run("sp"))

    def close(self):
        for cm in reversed(self._ctx):
            cm.__exit__(None, None, None)
        self._ctx = []


D = 1024
KC = 8
NMEM = 256
GW = 256
INC = 2976
DFF = 2816
NJ = 22
EPS = 1e-6
HG_STAGE = int(os.environ.get('HG_STAGE', '9'))
CH = 32


def vec_layout(L):
    per = [("g_mix", 8), ("a_w", 62), ("a_b", 2), ("a_lng", 2), ("a_lnb", 2), ("onorm", 2),
           ("c_w", 6), ("c_b", 2), ("qa_g", 2), ("kva_g", 1), ("qn_g", 1), ("kn_g", 1),
           ("gbr", 6), ("gbrD", 4), ("g_xq", 8), ("g_mem", 8), ("xqn", 1), ("xkn", 1), ("g_ffn", 8)]
    glob = [("hgam", 4 * L), ("eps", 1), ("pi", 1), ("invf", 1), ("hpi", 1), ("hm0", 1), ("hm1", 1)]
    idx = {}
    c = 0
    for k, n in glob:
        idx[k] = c
        c += n
    for l in range(L):
        for k, n in per:
            idx[(k, l)] = c
            c += n
    return idx, c


def pack_vecs(inp, L):
    idx, n = vec_layout(L)
    V = np.zeros((128, n), np.float32)

    def put(key, arr):
        arr = np.asarray(arr, np.float32)
        c0 = idx[key]
        V[:arr.shape[1], c0:c0 + arr.shape[0]] = arr.T

    hg = np.asarray(inp["h_gamma"], np.float32)
    put("hgam", hg.reshape(2 * L * 2, 128))
    V[:, idx["eps"]] = EPS
    V[:, idx["pi"]] = np.pi
    V[:, idx["hpi"]] = np.pi / 2
    V[0:64, idx["hm0"]] = 1.0
    V[64:128, idx["hm1"]] = 1.0
    inv = (10000.0 ** (-np.arange(0, 32, 2, dtype=np.float32) / 32)).astype(np.float32)
    inv = (inv / np.float32(2 * np.pi)).astype(np.float32)
    V[64:80, idx["invf"]] = inv
    V[80:96, idx["invf"]] = inv
    for l in range(L):
        put(("g_mix", l), inp["g_mix"][l].reshape(8, 128))
        aw = np.asarray(inp["a_dw_w"][l])
        put(("a_w", l), aw.reshape(31, 2, 128).transpose(1, 0, 2).reshape(62, 128))
        put(("a_b", l), inp["a_dw_b"][l].reshape(2, 128))
        put(("a_lng", l), inp["a_ln_g"][l].reshape(2, 128))
        put(("a_lnb", l), inp["a_ln_b"][l].reshape(2, 128))
        put(("onorm", l), inp["h_onorm_g"][l].reshape(2, 128))
        cw = np.asarray(inp["c_dw_w"][l])
        put(("c_w", l), cw.reshape(3, 2, 128).transpose(1, 0, 2).reshape(6, 128))
        put(("c_b", l), inp["c_dw_b"][l].reshape(2, 128))
        put(("qa_g", l), inp["m_qa_g"][l].reshape(2, 128))
        put(("kva_g", l), inp["m_kva_g"][l].reshape(1, 128))
        put(("qn_g", l), inp["m_qn_g"][l].reshape(1, 96))
        put(("kn_g", l), inp["m_kn_g"][l].reshape(1, 96))
        gb = np.asarray(inp["g_branch"][l])
        put(("gbr", l), gb[:768].reshape(6, 128))
        put(("gbrD", l), gb[768:].reshape(4, 64))
        put(("g_xq", l), inp["g_xq"][l].reshape(8, 128))
        put(("g_mem", l), inp["g_mem"][l].reshape(8, 128))
        put(("xqn", l), inp["x_qn_g"][l].reshape(1, 64))
        put(("xkn", l), inp["x_kn_g"][l].reshape(1, 64))
        put(("g_ffn", l), inp["g_ffn"][l].reshape(8, 128))
    return V


def const_mats():
    c = {}
    c["ident"] = np.eye(128, dtype=np.float32)
    rt = np.zeros((128, 128), np.float32)
    for j in range(16):
        rt[80 + j, 64 + j] = -1.0
        rt[64 + j, 80 + j] = 1.0
    c["rt"] = rt
    sh = np.zeros((128, 128), np.float32)
    for k in range(32):
        sh[k, 64 + k] = 1.0
    c["sh"] = sh
    bd = np.zeros((128, 128), np.float32)
    bd[:64, :64] = 1.0
    bd[64:, 64:] = 1.0
    c["bd"] = bd
    s = np.arange(128)[:, None]
    t = np.arange(128)[None, :]
    same = (s // CH) == (t // CH)
    mf = (same & (s <= t)).astype(np.float32)
    mb = (same & (s >= t)).astype(np.float32)
    rm = np.ones((128, 512), np.float32)
    rm[:, ::CH] = 0.0
    c["rmask"] = rm
    c["mf"] = np.concatenate([mf, mf], axis=1)
    c["mb"] = np.concatenate([mb, mb], axis=1)
    return c


class T:
    __slots__ = ("h", "b")

    def __init__(self, h, name):
        self.h = h
        self.b = Buf(name)

    def __getitem__(self, k):
        return self.h[k]


class Rot:
    def __init__(self, tiles):
        self.tiles = tiles
        self.i = 0

    def get(self):
        t = self.tiles[self.i]
        self.i = (self.i + 1) % len(self.tiles)
        return t


def _b(lst):
    return [t.b for t in lst]


class KB:
    def __init__(self, S, L, TT=512, debug=False):
        self.S, self.L, self.TT = S, L, TT
        self.NT = S // TT
        self.debug = debug
        nc = bass.Bass("TRN2", target_bir_lowering=False)
        self.nc = nc
        self.P = Prog(nc)
        self.P.setup_sems()
        self.idx, self.NV = vec_layout(L)
        self._scopes = []
        self._n = 0
        din = lambda name, shape, dt: nc.dram_tensor(name, shape, dt, kind="ExternalInput").ap()
        self.xin = din("xT", [D, S], F32)
        self.memT = din("memT", [D, NMEM], F32)
        self.pos = din("pos", [1, S], I32)
        self.vecs_d = din("vecs", [128, self.NV], F32)
        self.cm_d = {k: din("c_" + k, list(v.shape), F32) for k, v in const_mats().items()}
        self.w_in = din("w_in", [L, D, INC], F32)
        self.wuq = din("m_wuq", [L, 256, 384], F32)
        self.wukv = din("m_wukv", [L, 128, 512], F32)
        self.w_out = din("w_out", [L, D, D], F32)
        self.x_wq = din("x_wq", [L, D, 256], F32)
        self.x_wkv = din("x_wkv", [L, D, 512], F32)
        self.x_wo = din("x_wo", [L, 256, D], F32)
        self.f_w13 = din("f_w13", [L, D, 2 * DFF], F32)
        self.f_w2 = din("f_w2", [L, DFF, D], F32)
        self.y = nc.dram_tensor("y", [D, S], F32, kind="ExternalOutput").ap()
        kind = "ExternalOutput" if debug else "Internal"
        self.scr = {}
        self.dbufs = {}
        for name, shape, dt in [("hA", [256, S], BF16), ("qh", [256, S], BF16), ("lf", [2, 256, S], F32),
                                ("sg", [256, S], BF16), ("vh", [S, 256], BF16), ("mC", [256, S], BF16),
                                ("gbC", [256, S], BF16), ("cq", [256, S], BF16), ("ckv", [128, S], BF16),
                                ("kr", [32, S], BF16), ("cat", [D, S], BF16), ("Cf", [96, S], F32),
                                ("Sf", [96, S], F32)]:
            self.scr[name] = nc.dram_tensor("s_" + name, shape, dt, kind=kind).ap()
            self.dbufs[name] = [Buf(f"{name}{t}") for t in range(self.NT)]
        self.dbufs["xin"] = [Buf(f"xin{t}") for t in range(self.NT)]
        self.dbufs["y"] = [Buf(f"y{t}") for t in range(self.NT)]
        self.wbuf = Buf("weights_dram")

    def _name(self, s):
        self._n += 1
        return f"{s}_{self._n}"

    def sb(self, name, shape, dt):
        cm = self.nc.sbuf_tensor(self._name(name), shape, dt)
        h = cm.__enter__()
        self._scopes[-1].append(cm)
        return T(h, name)

    def psb(self, name, shape, dt):
        cm = self.nc.psum_tensor(self._name(name), shape, dt)
        h = cm.__enter__()
        self._scopes[-1].append(cm)
        return T(h, name)

    def rot(self, name, shape, dt, n, psum=False):
        return Rot([(self.psb if psum else self.sb)(f"{name}{i}", shape, dt) for i in range(n)])

    def push(self):
        self._scopes.append([])

    def pop(self):
        self.P.barrier()
        self.P.emit()
        for k in self.P.ops:
            self.P.ops[k] = []
        for cm in reversed(self._scopes.pop()):
            cm.__exit__(None, None, None)

    def mm(self, out, lhsT, rhs, start, stop, reads, writes):
        self.P.op("pe", lambda e: e.matmul(out, lhsT, rhs, start=start, stop=stop), _b(reads), _b(writes))

    def tr(self, out, in_, ident, reads, writes):
        self.P.op("pe", lambda e: e.transpose(out, in_, ident), _b(reads), _b(writes))

    def act(self, out, in_, func, reads, writes, bias=None, scale=1.0):
        if bias is None:
            self.P.op("act", lambda e: e.activation(out, in_, func, scale=scale), _b(reads), _b(writes))
        else:
            self.P.op("act", lambda e: e.activation(out, in_, func, bias=bias, scale=scale), _b(reads), _b(writes))

    def tt(self, eng, out, a, b, op, reads, writes):
        self.P.op(eng, lambda e: e.tensor_tensor(out, a, b, op), _b(reads), _b(writes))

    def ts(self, eng, out, a, s1, s2, op0, op1, reads, writes):
        if s2 is None:
            self.P.op(eng, lambda e: e.tensor_scalar(out, a, s1, None, op0), _b(reads), _b(writes))
        else:
            self.P.op(eng, lambda e: e.tensor_scalar(out, a, s1, s2, op0, op1), _b(reads), _b(writes))

    def stt(self, eng, out, a, sc, b, op0, op1, reads, writes):
        self.P.op(eng, lambda e: e.scalar_tensor_tensor(out, a, sc, b, op0, op1), _b(reads), _b(writes))

    def cp(self, eng, out, in_, reads, writes):
        if eng == "act":
            self.P.op("act", lambda e: e.copy(out, in_), _b(reads), _b(writes))
        else:
            self.P.op(eng, lambda e: e.tensor_copy(out, in_), _b(reads), _b(writes))

    def ms(self, eng, ap, val, writes):
        self.P.op(eng, lambda e: e.memset(ap, val), [], _b(writes))

    def dma(self, q, out, in_, reads, writes):
        self.P.dma(q, out, in_, _b(reads), _b(writes))

    def vc(self, key, c=0, p0=0, p1=128):
        i = self.idx[key] + c
        return self.vecs[p0:p1, i:i + 1]

    def rstd(self, out, ps, n, reads, writes, p0=0, p1=128):
        self.act(out, ps, AF.Ln, reads + [self.vecs], writes, bias=self.vc("eps", 0, p0, p1), scale=1.0 / n)
        self.act(out, out, AF.Exp, writes, writes, scale=-0.5)

    def setup(self):
        L, S, TT = self.L, self.S, self.TT
        self.push()
        self.vecs = self.sb("vecs", [128, self.NV], F32)
        self.dma("sp", self.vecs[:], self.vecs_d, [], [self.vecs])
        cmf = {}
        for k in ["ident", "rt", "sh", "bd"]:
            cmf[k] = self.sb("cf_" + k, [128, 128], F32)
            self.dma("sp", cmf[k][:], self.cm_d[k], [], [cmf[k]])
        self.bd_f = cmf["bd"]
        self.ident = self.sb("ident", [128, 128], BF16)
        self.rt = self.sb("rt", [128, 128], BF16)
        self.sh = self.sb("sh", [128, 128], BF16)
        self.bd = self.sb("bd", [128, 128], BF16)
        for k, t in [("ident", self.ident), ("rt", self.rt), ("sh", self.sh), ("bd", self.bd)]:
            self.cp("dve", t[:], cmf[k][:], [cmf[k]], [t])
        self.mf = self.sb("mf", [128, 256], F32)
        self.mb = self.sb("mb", [128, 256], F32)
        self.dma("sp", self.mf[:], self.cm_d["mf"], [], [self.mf])
        self.dma("sp", self.mb[:], self.cm_d["mb"], [], [self.mb])
        self.ones = self.sb("ones", [128, 128], BF16)
        self.ms("dve", self.ones[:], 1.0, [self.ones])
        self.ones_f = self.sb("ones_f", [128, 64], F32)
        self.ms("dve", self.ones_f[:], 1.0, [self.ones_f])
        nh = 4 * L
        hg0 = self.idx["hgam"]
        self.lb = self.sb("lb", [128, nh], F32)
        self.oml = self.sb("oml", [128, nh], F32)
        ex = self.sb("hg_e", [128, nh], F32)
        sm = self.sb("hg_s", [128, 4], F32)
        self.act(ex[:], self.vecs[:, hg0:hg0 + nh], AF.Exp, [self.vecs], [ex])
        e4 = ex[:].rearrange("p (d l c) -> p d l c", d=2, l=L, c=2)
        lb4 = self.lb[:].rearrange("p (d l c) -> p d l c", d=2, l=L, c=2)
        s3 = sm[:].rearrange("p (d c) -> p d c", d=2, c=2)
        self.cp("dve", s3, e4[:, :, 0, :], [ex], [sm])
        for l in range(1, L):
            self.tt("dve", s3, s3, e4[:, :, l, :], ALU.add, [sm, ex], [sm])
        self.P.op("dve", lambda e: e.reciprocal(sm[:], sm[:]), [sm.b], [sm.b])
        self.ms("dve", self.lb[:], 0.0, [self.lb])
        for l in range(1, L):
            self.tt("dve", lb4[:, :, l, :], e4[:, :, l, :], s3, ALU.mult, [ex, sm, self.lb], [self.lb])
            self.tt("dve", lb4[:, :, l, :], lb4[:, :, l, :], lb4[:, :, l - 1, :], ALU.add, [self.lb], [self.lb])
        self.ts("dve", self.oml[:], self.lb[:], -1.0, 1.0, ALU.mult, ALU.add, [self.lb], [self.oml])
        self.push()
        pi_ = self.sb("pos_i", [96, TT], I32)
        pf = self.sb("pos_f", [96, TT], F32)
        ang = self.sb("ang", [96, TT], F32)
        r_cf = self.rot("cft", [96, TT], F32, 2)
        r_sf = self.rot("sft", [96, TT], F32, 2)
        twopi = float(2 * np.pi)
        ki = self.sb("ang_i", [96, TT], I32)
        kf = self.sb("ang_kf", [96, TT], F32)
        tq = self.sb("ang_t", [96, TT], F32)
        R_ = slice(64, 96)

        def reduce_turns(u):
            self.cp("dve", ki[R_, :], u[R_, :], [u], [ki])
            self.cp("dve", kf[R_, :], ki[R_, :], [ki], [kf])
            self.tt("dve", u[R_, :], u[R_, :], kf[R_, :], ALU.subtract, [u, kf], [u])
            self.ts("dve", tq[R_, :], u[R_, :], 0.5, None, ALU.is_gt, None, [u], [tq])
            self.tt("dve", u[R_, :], u[R_, :], tq[R_, :], ALU.subtract, [u, tq], [u])
            self.ts("dve", tq[R_, :], u[R_, :], -0.5, None, ALU.is_lt, None, [u], [tq])
            self.tt("dve", u[R_, :], u[R_, :], tq[R_, :], ALU.add, [u, tq], [u])

        for t in range(self.NT):
            sl = slice(t * TT, (t + 1) * TT)
            self.dma("sp", pi_[64:96, :], self.pos[0:1, sl].to_broadcast([32, TT]), [], [pi_])
            self.cp("dve", pf[R_, :], pi_[R_, :], [pi_], [pf])
            self.ts("dve", pf[R_, :], pf[R_, :], self.vc("invf", 0, 64, 96), None, ALU.mult, None, [pf, self.vecs], [pf])
            self.ts("dve", ang[R_, :], pf[R_, :], 0.25, None, ALU.add, None, [pf], [ang])
            cf = r_cf.get()
            sf = r_sf.get()
            self.ms("dve", cf[0:64, :], 1.0, [cf])
            self.ms("dve", sf[0:64, :], 0.0, [sf])
            reduce_turns(pf)
            self.act(sf[R_, :], pf[R_, :], AF.Sin, [pf], [sf], scale=twopi)
            reduce_turns(ang)
            self.act(cf[R_, :], ang[R_, :], AF.Sin, [ang], [cf], scale=twopi)
            self.P.dma("pool", self.scr["Cf"][:, sl], cf[:], [cf.b], [self.dbufs["Cf"][t]])
            self.P.dma("pool", self.scr["Sf"][:, sl], sf[:], [sf.b], [self.dbufs["Sf"][t]])
        self.pop()

    def load_x(self, xsrc, xname, t, xt):
        TT = self.TT
        src = xsrc.rearrange("(c p) s -> p c s", p=128)[:, :, t * TT:(t + 1) * TT]
        self.P.dma("sp", xt[:], src, [self.dbufs[xname][t]], [xt.b])

    def norm_tile(self, xt, gkey, l, sq, rs, nT, pst):
        TT = self.TT
        self.act(sq[:], xt[:], AF.Square, [xt], [sq])
        for c in range(KC):
            self.mm(pst[:, :TT], self.ones[:], sq[:, c, :], c == 0, c == KC - 1, [sq, self.ones], [pst])
        self.rstd(rs[:], pst[:, :TT], D, [pst], [rs])
        for c in range(KC):
            self.stt("dve", nT[:, c, :], xt[:, c, :], self.vc((gkey, l), c), rs[:], ALU.mult, ALU.mult,
                     [xt, rs, self.vecs], [nT])

    def group_norm_store(self, ys, gcols, row0, t, pst, r_sq, rs, r_ob, npart=128):
        TT = self.TT
        n = len(ys)
        for i, (yt, yap) in enumerate(ys):
            sq = r_sq.get()
            self.act(sq[0:npart, :], yap, AF.Square, [yt], [sq])
            self.mm(pst[0:npart, :TT], self.ones[0:npart, 0:npart], sq[0:npart, :], i == 0, i == n - 1,
                    [sq, self.ones], [pst])
        self.rstd(rs[0:npart, :], pst[0:npart, :TT], GW, [pst], [rs], 0, npart)
        for i, (yt, yap) in enumerate(ys):
            ob = r_ob.get()
            self.stt("dve", ob[0:npart, :], yap, gcols[i], rs[0:npart, :], ALU.mult, ALU.mult,
                     [yt, rs, self.vecs], [ob])
            r0 = row0 + i * npart
            self.P.dma("pool", self.scr["cat"][r0:r0 + npart, t * TT:(t + 1) * TT], ob[0:npart, :],
                       [ob.b], [self.dbufs["cat"][t]])

    def ph_inproj(self, l, xsrc, xname):
        TT, NT = self.TT, self.NT
        self.push()
        w = [self.sb(f"w_in{kc}", [128, INC], BF16) for kc in range(KC)]
        for kc in range(KC):
            self.P.dma("pool", w[kc][:], self.w_in[l, kc * 128:(kc + 1) * 128, :], [self.wbuf], [w[kc].b])
        r_xt = self.rot("xt", [128, KC, TT], F32, 2)
        r_sq = self.rot("sq", [128, KC, TT], BF16, 1)
        r_rs = self.rot("rs", [128, TT], F32, 2)
        r_nT = self.rot("nT", [128, KC, TT], BF16, 2)
        r_ps = self.rot("ps", [128, 512], F32, 6, psum=True)
        r_pst = self.rot("pst", [128, 512], F32, 1, psum=True)
        r_ob = self.rot("ob", [128, TT], BF16, 4)
        r_of = self.rot("of", [128, TT], F32, 4)
        r_vt = self.rot("vt", [128, 256], BF16, 2)
        sl_t = lambda t: slice(t * TT, (t + 1) * TT)
        lbc = lambda d, cc: ((d * self.L + l) * 2 + cc)

        def proj(col0, ncol, nT):
            ps = r_ps.get()
            for kc in range(KC):
                self.mm(ps[0:ncol, :TT], w[kc][:, col0:col0 + ncol], nT[:, kc, :], kc == 0, kc == KC - 1,
                        [w[kc], nT], [ps])
            return ps

        def store(name, row0, nrow, t, ob):
            dst = self.scr[name]
            self.P.dma("pool", dst[row0:row0 + nrow, sl_t(t)], ob[0:nrow, :], [ob.b], [self.dbufs[name][t]])

        for t in range(NT):
            xt = r_xt.get()
            self.load_x(xsrc, xname, t, xt)
            sq, rs, nT = r_sq.get(), r_rs.get(), r_nT.get()
            self.norm_tile(xt, "g_mix", l, sq, rs, nT, r_pst.get())
            for j in range(2):
                psg = proj((2 + j) * 128, 128, nT)
                sg = r_of.get()
                self.act(sg[:], psg[:, :TT], AF.Sigmoid, [psg], [sg])
                psa = proj(j * 128, 128, nT)
                ob = r_ob.get()
                self.tt("dve", ob[:], psa[:, :TT], sg[:], ALU.mult, [psa, sg], [ob])
                store("hA", j * 128, 128, t, ob)
            for j in range(2):
                ps = proj(512 + j * 128, 128, nT)
                ob = r_ob.get()
                self.cp("dve", ob[:], ps[:, :TT], [ps], [ob])
                store("qh", j * 128, 128, t, ob)
            for d in range(2):
                for j in range(2):
                    ps = proj(768 + d * 256 + j * 128, 128, nT)
                    of = r_of.get()
                    self.act(of[:], ps[:, :TT], AF.Sigmoid, [ps], [of])
                    of2 = r_of.get()
                    self.act(of2[:], of[:], AF.Ln, [of, self.oml, self.lb], [of2],
                             bias=self.lb[:, lbc(d, j):lbc(d, j) + 1], scale=self.oml[:, lbc(d, j):lbc(d, j) + 1])
                    self.P.dma("pool", self.scr["lf"][d, j * 128:(j + 1) * 128, sl_t(t)], of2[:], [of2.b],
                               [self.dbufs["lf"][t]])
            for j in range(2):
                ps = proj(1536 + j * 128, 128, nT)
                ob = r_ob.get()
                self.act(ob[:], ps[:, :TT], AF.Silu, [ps], [ob])
                store("sg", j * 128, 128, t, ob)
            for tb in range(TT // 128):
                ps = r_ps.get()
                for kc in range(KC):
                    self.mm(ps[:, 0:256], nT[:, kc, tb * 128:(tb + 1) * 128], w[kc][:, 1280:1536], kc == 0,
                            kc == KC - 1, [w[kc], nT], [ps])
                vt = r_vt.get()
                self.cp("act", vt[:], ps[:, 0:256], [ps], [vt])
                r0 = t * TT + tb * 128
                self.P.dma("pool", self.scr["vh"][r0:r0 + 128, :], vt[:], [vt.b], [self.dbufs["vh"][t]])
            for j in range(2):
                ps = proj(1792 + j * 128, 128, nT)
                ob = r_ob.get()
                self.cp("act", ob[:], ps[:, :TT], [ps], [ob])
                store("gbC", j * 128, 128, t, ob)
                psc = proj(2048 + j * 128, 128, nT)
                of = r_of.get()
                self.cp("act", of[:], psc[:, :TT], [psc], [of])
                psx = proj(2304 + j * 128, 128, nT)
                ob = r_ob.get()
                self.tt("dve", ob[:], psx[:, :TT], of[:], ALU.mult, [psx, of], [ob])
                store("mC", j * 128, 128, t, ob)
            for j in range(2):
                ps = proj(2560 + j * 128, 128, nT)
                ob = r_ob.get()
                self.cp("dve", ob[:], ps[:, :TT], [ps], [ob])
                store("cq", j * 128, 128, t, ob)
            ps = proj(2816, 128, nT)
            ob = r_ob.get()
            self.cp("act", ob[:], ps[:, :TT], [ps], [ob])
            store("ckv", 0, 128, t, ob)
            ps = proj(2944, 32, nT)
            ob = r_ob.get()
            self.cp("dve", ob[0:32, :], ps[0:32, :TT], [ps], [ob])
            store("kr", 0, 32, t, ob)
        self.pop()

    def ph_conv(self, l, kind):
        TT, NT, S = self.TT, self.NT, self.S
        W = 31 if kind == "A" else 3
        H = W // 2
        src_name = "hA" if kind == "A" else "mC"
        wkey, bkey = ("a_w", "a_b") if kind == "A" else ("c_w", "c_b")
        self.push()
        dg = [self.sb(f"dg{i}", [128, 128], BF16) for i in range(2 * W)]
        for i in range(2 * W):
            self.ts("dve", dg[i][:], self.ident[:], self.vc((wkey, l), i), None, ALU.mult, None,
                    [self.ident, self.vecs], [dg[i]])
        r_win = self.rot("win", [128, TT + 2 * H], BF16, 4)
        r_ps = self.rot("ps", [128, 512], F32, 3, psum=True)
        r_pst = self.rot("pst", [128, 512], F32, 3, psum=True)
        cb = [self.rot(f"c32_{cc}", [128, TT], F32, 2) for cc in range(2)]
        yb = [self.rot(f"y32_{cc}", [128, TT], F32, 2) for cc in range(2)]
        r_tmp = self.rot("tmp", [128, TT], F32, 4)
        r_sq = self.rot("sqb", [128, TT], BF16, 2)
        r_rs = self.rot("rs", [128, TT], F32, 2)
        r_ob = self.rot("ob", [128, TT], BF16, 4)
        r_gb = self.rot("gb", [128, TT], BF16, 2)
        onesf = self.sb("onesf", [128, 128], F32)
        self.ms("dve", onesf[:], 1.0, [onesf])
        for t in range(NT):
            lo, hi = t * TT - H, t * TT + TT + H
            clo, chi = max(lo, 0), min(hi, S)
            rd = [self.dbufs[src_name][tt_] for tt_ in (t - 1, t, t + 1) if 0 <= tt_ < NT]
            cs = []
            for cc in range(2):
                win = r_win.get()
                if clo != lo or chi != hi:
                    self.ms("pool", win[:], 0.0, [win])
                self.P.dma("sp", win[:, clo - lo:chi - lo], self.scr[src_name][cc * 128:(cc + 1) * 128, clo:chi],
                           rd, [win.b])
                ps = r_ps.get()
                for tap in range(W):
                    self.mm(ps[:, :TT], dg[cc * W + tap][:], win[:, tap:tap + TT], tap == 0, tap == W - 1,
                            [dg[cc * W + tap], win], [ps])
                c32 = cb[cc].get()
                self.act(c32[:], ps[:, :TT], AF.Identity, [ps, self.vecs], [c32], bias=self.vc((bkey, l), cc))
                cs.append(c32)
            ys = []
            if kind == "A":
                psm, psq = r_pst.get(), r_pst.get()
                for cc in range(2):
                    self.mm(psm[:, :TT], onesf[:], cs[cc][:], cc == 0, cc == 1, [onesf, cs[cc]], [psm])
                for cc in range(2):
                    sq = r_tmp.get()
                    self.act(sq[:], cs[cc][:], AF.Square, [cs[cc]], [sq])
                    self.mm(psq[:, :TT], onesf[:], sq[:], cc == 0, cc == 1, [onesf, sq], [psq])
                mean = r_tmp.get()
                self.act(mean[:], psm[:, :TT], AF.Identity, [psm], [mean], scale=1.0 / GW)
                msq = r_tmp.get()
                self.tt("dve", msq[:], mean[:], mean[:], ALU.mult, [mean], [msq])
                var = r_rs.get()
                self.stt("dve", var[:], psq[:, :TT], 1.0 / GW, msq[:], ALU.mult, ALU.subtract, [psq, msq], [var])
                self.rstd(var[:], var[:], 1.0, [var], [var])
                for cc in range(2):
                    t1 = r_tmp.get()
                    self.tt("dve", t1[:], cs[cc][:], mean[:], ALU.subtract, [cs[cc], mean], [t1])
                    self.tt("dve", t1[:], t1[:], var[:], ALU.mult, [t1, var], [t1])
                    y = yb[cc].get()
                    self.act(y[:], t1[:], AF.Silu, [t1, self.vecs], [y], bias=self.vc(("a_lnb", l), cc),
                             scale=self.vc(("a_lng", l), cc))
                    ys.append((y, y[:]))
                gcols = [self.vc(("gbr", l), 0), self.vc(("gbr", l), 1)]
                row0 = 0
            else:
                for cc in range(2):
                    gb = r_gb.get()
                    self.P.dma("sp", gb[:], self.scr["gbC"][cc * 128:(cc + 1) * 128, t * TT:(t + 1) * TT],
                               [self.dbufs["gbC"][t]], [gb.b])
                    y = yb[cc].get()
                    self.tt("dve", y[:], cs[cc][:], gb[:], ALU.mult, [cs[cc], gb], [y])
                    ys.append((y, y[:]))
                gcols = [self.vc(("gbr", l), 4), self.vc(("gbr", l), 5)]
                row0 = 512
            self.group_norm_store(ys, gcols, row0, t, r_pst.get(), r_sq, r_rs.get(), r_ob)
        self.pop()

    def head_norm_rope(self, ps, n, gcol, cf, sf, out_ap, out_T, R, W=None):
        TT = self.TT if W is None else W
        raw = R["raw"].get()
        self.cp("act", raw[0:n, :TT], ps[0:n, :TT], [ps], [raw])
        sq = R["sqh"].get()
        self.act(sq[0:n, :TT], ps[0:n, :TT], AF.Square, [ps], [sq])
        pss = R["pst"].get()
        self.mm(pss[0:n, :TT], self.ones[0:n, 0:n], sq[0:n, :TT], True, True, [self.ones, sq], [pss])
        rs = R["rsh"].get()
        self.rstd(rs[0:n, :TT], pss[0:n, :TT], n, [pss], [rs], 0, n)
        if cf is None:
            self.stt("dve", out_ap, raw[0:n, :TT], gcol, rs[0:n, :TT], ALU.mult, ALU.mult, [raw, rs, self.vecs], [out_T])
            return
        qn = R["qn"].get()
        self.stt("dve", qn[0:n, :], raw[0:n, :], gcol, rs[0:n, :], ALU.mult, ALU.mult, [raw, rs, self.vecs], [qn])
        qnb = R["qnb"].get()
        self.cp("pool", qnb[0:n, :], qn[0:n, :], [qn], [qnb])
        psr = R["pst"].get()
        self.mm(psr[0:n, :TT], self.rt[0:n, 0:n], qnb[0:n, :], True, True, [self.rt, qnb], [psr])
        t1 = R["raw"].get()
        self.tt("dve", t1[0:n, :], qn[0:n, :], cf[0:n, :], ALU.mult, [qn, cf], [t1])
        t2 = R["qn"].get()
        self.tt("dve", t2[0:n, :], psr[0:n, :TT], sf[0:n, :], ALU.mult, [psr, sf], [t2])
        self.tt("dve", out_ap, t1[0:n, :], t2[0:n, :], ALU.add, [t1, t2], [out_T])

    def hnr_multi(self, pss_in, n, gcol, cf, sf, outs, R, r_ps, W=None):
        TT = self.TT if W is None else W
        H = len(pss_in)
        raws, sqs, ps2, rss = [], [], [], []
        for h in range(H):
            raw = R["raw"].get()
            self.cp("act", raw[0:n, :TT], pss_in[h][0:n, :TT], [pss_in[h]], [raw])
            sq = R["sqh"].get()
            self.act(sq[0:n, :TT], pss_in[h][0:n, :TT], AF.Square, [pss_in[h]], [sq])
            raws.append(raw)
            sqs.append(sq)
        for h in range(H):
            p = r_ps.get()
            self.mm(p[0:n, :TT], self.ones[0:n, 0:n], sqs[h][0:n, :TT], True, True, [self.ones, sqs[h]], [p])
            ps2.append(p)
        for h in range(H):
            rs = R["rsh"].get()
            self.act(rs[0:n, :TT], ps2[h][0:n, :TT], AF.Ln, [ps2[h], self.vecs], [rs], bias=self.vc("eps", 0, 0, n), scale=1.0 / n)
            rss.append(rs)
        for h in range(H):
            self.act(rss[h][0:n, :TT], rss[h][0:n, :TT], AF.Exp, [rss[h]], [rss[h]], scale=-0.5)
        if cf is None:
            for h in range(H):
                self.stt("dve", outs[h][0], raws[h][0:n, :TT], gcol, rss[h][0:n, :TT], ALU.mult, ALU.mult,
                         [raws[h], rss[h], self.vecs], [outs[h][1]])
            return
        qns, qnbs, psrs, t1s, t2s = [], [], [], [], []
        for h in range(H):
            qn = R["qn"].get()
            self.stt("dve", qn[0:n, :], raws[h][0:n, :], gcol, rss[h][0:n, :], ALU.mult, ALU.mult, [raws[h], rss[h], self.vecs], [qn])
            qns.append(qn)
        for h in range(H):
            qnb = R["qnb"].get()
            self.cp("pool", qnb[0:n, :], qns[h][0:n, :], [qns[h]], [qnb])
            qnbs.append(qnb)
        for h in range(H):
            psr = r_ps.get()
            self.mm(psr[0:n, :TT], self.rt[0:n, 0:n], qnbs[h][0:n, :], True, True, [self.rt, qnbs[h]], [psr])
            psrs.append(psr)
        for h in range(H):
            t1 = R["raw"].get()
            self.tt("pool", t1[0:n, :], qns[h][0:n, :], cf[0:n, :], ALU.mult, [qns[h], cf], [t1])
            t1s.append(t1)
        for h in range(H):
            t2 = R["qn"].get()
            self.tt("dve", t2[0:n, :], psrs[h][0:n, :TT], sf[0:n, :], ALU.mult, [psrs[h], sf], [t2])
            t2s.append(t2)
        for h in range(H):
            self.tt("dve", outs[h][0], t1s[h][0:n, :], t2s[h][0:n, :], ALU.add, [t1s[h], t2s[h]], [outs[h][1]])

    def softmax_av_finish(self, pso, h, R, yD):
        TT = self.TT
        rd = R["rd"].get()
        self.P.op("dve", lambda e: e.reciprocal(rd[64:65, :], pso[64:65, :TT]), [pso.b], [rd.b])
        psb = R["pst"].get()
        self.mm(psb[0:64, :TT], self.ones_f[64:65, 0:64], rd[64:65, :], True, True, [self.ones_f, rd], [psb])
        osb = R["osb"].get()
        self.cp("act", osb[0:64, :], pso[0:64, :TT], [pso], [osb])
        self.tt("dve", yD[h][0:64, :], osb[0:64, :], psb[0:64, :TT], ALU.mult, [osb, psb], [yD[h]])

    def ph_mla(self, l):
        TT, NT, S = self.TT, self.NT, self.S
        NKB = S // 128
        self.push()
        wuq = self.sb("wuq", [128, 2, 384], BF16)
        self.P.dma("pool", wuq[:], self.wuq[l].rearrange("(c p) n -> p c n", p=128), [self.wbuf], [wuq.b])
        Wk = self.sb("Wk", [128, 4, 96], BF16)
        self.ms("dve", Wk[:], 0.0, [Wk])
        wkv4 = self.wukv[l].rearrange("p (h x) -> p h x", h=4)
        self.P.dma("pool", Wk[:, :, 0:64], wkv4[:, :, 0:64], [self.wbuf], [Wk.b])
        Wv = self.sb("Wv", [128, 4, 64], BF16)
        self.P.dma("pool", Wv[:], wkv4[:, :, 64:128], [self.wbuf], [Wv.b])
        KT = [self.sb(f"KT{h}", [96, S], BF16) for h in range(4)]
        Va = self.sb("Vaug", [128, NKB, 4, 65], BF16)
        self.ms("pool", Va[:, :, :, 64:65], 1.0, [Va])
        R = {"raw": self.rot("raw", [96, TT], F32, 8), "sqh": self.rot("sqh", [96, TT], BF16, 4),
             "pst": self.rot("pst", [128, 512], F32, 2, psum=True), "rsh": self.rot("rsh", [96, TT], F32, 4),
             "qn": self.rot("qn", [96, TT], F32, 8), "qnb": self.rot("qnb", [96, TT], BF16, 4),
             "rd": self.rot("rd", [65, TT], F32, 2), "osb": self.rot("osb", [64, TT], F32, 2)}
        r_ps = self.rot("ps", [128, 512], F32, 4, psum=True)
        r_pso = self.rot("pso", [128, 512], F32, 2, psum=True)
        r_cf = self.rot("cf", [96, TT], F32, 2)
        r_sf = self.rot("sf", [96, TT], F32, 2)
        r_in = self.rot("ckv", [128, TT], BF16, 2)
        r_kr = self.rot("kr", [32, TT], BF16, 2)
        r_sq = self.rot("sq", [128, TT], BF16, 2)
        r_rs = self.rot("rs", [128, TT], F32, 2)
        r_n = self.rot("ckvn", [128, TT], BF16, 2)
        r_cq = self.rot("cq", [128, 2, TT], BF16, 2)
        r_cqs = self.rot("cqs", [128, 2, TT], BF16, 1)
        r_cqn = self.rot("cqn", [128, 2, TT], BF16, 2)
        r_q = self.rot("Qh", [96, TT], BF16, 5)
        r_pe = self.rot("pe", [128, TT], BF16, 4)
        yD = [self.sb(f"yD{h}", [64, TT], F32) for h in range(4)]
        r_sqg = self.rot("sqg", [128, TT], BF16, 2)
        r_ob = self.rot("ob", [128, TT], BF16, 4)
        sl_t = lambda t: slice(t * TT, (t + 1) * TT)

        def load_cs(t):
            cf, sf = r_cf.get(), r_sf.get()
            self.P.dma("sp", cf[:], self.scr["Cf"][:, sl_t(t)], [self.dbufs["Cf"][t]], [cf.b])
            self.P.dma("sp", sf[:], self.scr["Sf"][:, sl_t(t)], [self.dbufs["Sf"][t]], [sf.b])
            return cf, sf

        for t in range(NT):
            ckv, kr = r_in.get(), r_kr.get()
            self.P.dma("sp", ckv[:], self.scr["ckv"][:, sl_t(t)], [self.dbufs["ckv"][t]], [ckv.b])
            self.P.dma("sp", kr[:], self.scr["kr"][:, sl_t(t)], [self.dbufs["kr"][t]], [kr.b])
            cf, sf = load_cs(t)
            sq = r_sq.get()
            self.act(sq[:], ckv[:], AF.Square, [ckv], [sq])
            pss = R["pst"].get()
            self.mm(pss[:, :TT], self.ones[:], sq[:], True, True, [self.ones, sq], [pss])
            rs = r_rs.get()
            self.rstd(rs[:], pss[:, :TT], 128, [pss], [rs])
            cn = r_n.get()
            self.stt("dve", cn[:], ckv[:], self.vc(("kva_g", l)), rs[:], ALU.mult, ALU.mult, [ckv, rs, self.vecs], [cn])
            psks = []
            for h in range(4):
                psk = r_ps.get()
                self.mm(psk[0:96, :TT], Wk[:, h, :], cn[:], True, False, [Wk, cn], [psk])
                self.mm(psk[0:96, :TT], self.sh[0:32, 0:96], kr[0:32, :], False, True, [self.sh, kr], [psk])
                psks.append(psk)
            self.hnr_multi(psks, 96, self.vc(("kn_g", l), 0, 0, 96), cf, sf,
                           [(KT[h][0:96, sl_t(t)], KT[h]) for h in range(4)], R, r_ps)
            for tb in range(TT // 128):
                psv = r_ps.get()
                self.mm(psv[:, 0:256], cn[:, tb * 128:(tb + 1) * 128], Wv[:].rearrange("p h e -> p (h e)"), True, True,
                        [cn, Wv], [psv])
                kb = t * (TT // 128) + tb
                self.cp("act", Va[:, kb, :, 0:64], psv[:, 0:256].rearrange("p (h e) -> p h e", h=4), [psv], [Va])
        sc = 96.0 ** -0.5
        for t in range(NT):
            cq = r_cq.get()
            self.P.dma("sp", cq[:], self.scr["cq"].rearrange("(c p) s -> p c s", p=128)[:, :, sl_t(t)],
                       [self.dbufs["cq"][t]], [cq.b])
            cf, sf = load_cs(t)
            cqs = r_cqs.get()
            self.act(cqs[:], cq[:], AF.Square, [cq], [cqs])
            pss = R["pst"].get()
            for c in range(2):
                self.mm(pss[:, :TT], self.ones[:], cqs[:, c, :], c == 0, c == 1, [self.ones, cqs], [pss])
            rs = r_rs.get()
            self.rstd(rs[:], pss[:, :TT], 256, [pss], [rs])
            cqn = r_cqn.get()
            for c in range(2):
                self.stt("dve", cqn[:, c, :], cq[:, c, :], self.vc(("qa_g", l), c), rs[:], ALU.mult, ALU.mult,
                         [cq, rs, self.vecs], [cqn])
            Qs, psqs = [], []
            for h in range(4):
                psq = r_ps.get()
                for c in range(2):
                    self.mm(psq[0:96, :TT], wuq[:, c, h * 96:(h + 1) * 96], cqn[:, c, :], c == 0, c == 1, [wuq, cqn], [psq])
                psqs.append(psq)
                Qs.append(r_q.get())
            self.hnr_multi(psqs, 96, self.vc(("qn_g", l), 0, 0, 96), cf, sf, [(Qs[h][0:96, :], Qs[h]) for h in range(4)], R, r_ps)
            pending = None
            for h in range(4):
                Qh = Qs[h]
                pso = r_pso.get()

                def scores(kb, h=h, Qh=Qh):
                    p = r_ps.get()
                    self.mm(p[:, :TT], KT[h][0:96, kb * 128:(kb + 1) * 128], Qh[0:96, :], True, True, [KT[h], Qh], [p])
                    return p

                LA = 2
                q_sc = [scores(kb) for kb in range(min(LA, NKB))]
                if pending is not None:
                    self.softmax_av_finish(*pending)
                    pending = None
                for kb in range(NKB):
                    pssc = q_sc.pop(0)
                    if kb + LA < NKB:
                        q_sc.append(scores(kb + LA))
                    pe = r_pe.get()
                    self.act(pe[:], pssc[:, :TT], AF.Exp, [pssc], [pe], scale=sc)
                    self.mm(pso[0:65, :TT], Va[:, kb, h, :], pe[:], kb == 0, kb == NKB - 1, [Va, pe], [pso])
                pending = (pso, h, R, yD)
            self.softmax_av_finish(*pending)
            self.group_norm_store([(yD[h], yD[h][0:64, :]) for h in range(4)],
                                  [self.vc(("gbrD", l), h, 0, 64) for h in range(4)], 768, t,
                                  R["pst"].get(), r_sqg, r_rs.get(), r_ob, npart=64)
        self.pop()

    def ph_hgrn(self, l):
        TT, NT, S = self.TT, self.NT, self.S
        NB = TT // 128
        NCB = 128 // CH
        NCT = TT // CH
        self.push()
        oacc = [self.sb(f"oacc{hp}", [128, S], F32) for hp in range(2)]
        oaccb = [self.sb(f"oaccb{hp}", [128, S], F32) for hp in range(2)]
        r_sst = [self.rot(f"Sst{hp}", [128, 128], F32, 4) for hp in range(2)]
        r_sbf = [self.rot(f"Sbf{hp}", [128, 128], BF16, 4) for hp in range(2)]
        PADW = 16
        PW = CH + PADW
        v3 = lambda ap: ap.rearrange("p (c j) -> p c j", j=CH)
        r_lf = self.rot("lf", [128, NCT, PW], F32, 2)
        for tl in r_lf.tiles:
            self.ms("dve", tl[:, :, 0:PADW], 0.0, [tl])
        cs_a = self.sb("cs_a", [128, NCT, PW], F32)
        cs_b = self.sb("cs_b", [128, NCT, PW], F32)
        self.ms("dve", cs_a[:, :, 0:PADW], 0.0, [cs_a])
        self.ms("dve", cs_b[:, :, 0:PADW], 0.0, [cs_b])

        def shadd(dst_ap, src, sh, dst_T):
            self.tt("pool", dst_ap, src[:, :, PADW:PW], src[:, :, PADW - sh:PW - sh], ALU.add, [src], [dst_T])

        r_q = self.rot("q", [128, TT], BF16, 2)
        r_b = self.rot("b", [128, TT], F32, 3)
        r_f = self.rot("f32t", [128, TT], F32, 6)
        r_qt = self.rot("qt", [128, TT], BF16, 5)
        r_kt = self.rot("kt", [128, TT], BF16, 9)
        r_kh = self.rot("kh", [128, TT], BF16, 5)
        r_eb = self.rot("eb", [128, TT], F32, 5)
        r_tp = self.rot("tp", [128, NCB, 128], BF16, 1, psum=True)
        r_pat = self.rot("pat", [128, 512], F32, 1, psum=True)
        r_pio = self.rot("pio", [128, 512], F32, 3, psum=True)
        r_pu = self.rot("pu4", [128, 512], F32, 3, psum=True)
        r_kht = self.rot("kht", [32, NCB, 128], BF16, 3)
        r_vjt = self.rot("vjt", [32, NCT, 128], BF16, 5)
        r_vbt = self.rot("vbt", [128, NB, 128], BF16, 5)
        r_atm = self.rot("atm", [128, 256], BF16, 2)
        r_isb = self.rot("isb", [128, 128], F32, 2)
        r_tmp = self.rot("ctmp", [128, 128], F32, 2)
        sl_t = lambda t: slice(t * TT, (t + 1) * TT)
        for d in range(2):
            fwd = d == 0
            mask = self.mf if fwd else self.mb
            di = CH - 1 if fwd else 0
            cur = []
            curS = []
            for hp in range(2):
                z0 = r_sst[hp].get()
                self.ms("dve", z0[:], 0.0, [z0])
                curS.append(z0)
                z = r_sbf[hp].get()
                self.ms("pool", z[:], 0.0, [z])
                cur.append(z)
            prepd = {}

            def prep(t, d=d, fwd=fwd, di=di, prepd=prepd):
                for hp in range(2):
                    rows = slice(hp * 128, (hp + 1) * 128)
                    lf, q = r_lf.get(), r_q.get()
                    lfv = lf[:, :, PADW:PW]
                    self.P.dma("sp", lfv, v3(self.scr["lf"][d, rows, sl_t(t)]), [self.dbufs["lf"][t]], [lf.b])
                    self.P.dma("sp", q[:], self.scr["qh"][rows, sl_t(t)], [self.dbufs["qh"][t]], [q.b])
                    b = r_b.get()
                    shadd(cs_a[:, :, PADW:PW], lf, 1, cs_a)
                    shadd(cs_b[:, :, PADW:PW], cs_a, 2, cs_b)
                    shadd(cs_a[:, :, PADW:PW], cs_b, 4, cs_a)
                    shadd(cs_b[:, :, PADW:PW], cs_a, 8, cs_b)
                    shadd(v3(b[:]), cs_b, 16, b)
                    if not fwd:
                        b3 = v3(b[:])
                        tmp = r_f.get()
                        self.tt("pool", v3(tmp[:]), b3[:, :, CH - 1:CH].to_broadcast([128, NCT, CH]), b3, ALU.subtract, [b], [tmp])
                        bb = r_b.get()
                        self.tt("pool", v3(bb[:]), v3(tmp[:]), lfv, ALU.add, [tmp, lf], [bb])
                        b = bb
                    eb, enb, ef = r_eb.get(), r_f.get(), r_f.get()
                    self.act(eb[:], b[:], AF.Exp, [b], [eb])
                    self.act(enb[:], b[:], AF.Exp, [b], [enb], scale=-1.0)
                    self.act(v3(ef[:]), lfv, AF.Exp, [lf], [ef])
                    k = r_f.get()
                    self.ts("dve", k[:], ef[:], -1.0, 1.0, ALU.mult, ALU.add, [ef], [k])
                    qt = r_qt.get()
                    self.tt("dve", qt[:], q[:], eb[:], ALU.mult, [q, eb], [qt])
                    kt32 = r_f.get()
                    self.tt("dve", kt32[:], k[:], enb[:], ALU.mult, [k, enb], [kt32])
                    kt = []
                    for h2 in range(2):
                        km = r_kt.get()
                        self.ts("dve", km[:], kt32[:], self.vc("hm%d" % h2), None, ALU.mult, None, [kt32, self.vecs], [km])
                        kt.append(km)
                    eb3 = v3(eb[:])
                    kh = r_kh.get()
                    self.tt("dve", v3(kh[:]), v3(kt32[:]), eb3[:, :, di:di + 1].to_broadcast([128, NCT, CH]), ALU.mult, [kt32, eb], [kh])
                    cols = slice(hp * 128, (hp + 1) * 128)
                    vjt, vbt = r_vjt.get(), r_vbt.get()
                    vsrc = self.scr["vh"][t * TT:(t + 1) * TT, cols]
                    self.P.dma("sp", vjt[:], vsrc.rearrange("(c j) e -> j c e", j=CH), [self.dbufs["vh"][t]], [vjt.b])
                    self.P.dma("sp", vbt[:], vsrc.rearrange("(b p) e -> p b e", p=128), [self.dbufs["vh"][t]], [vbt.b])
                    prepd[(t, hp)] = (qt, kt, kh, eb, vjt, vbt)

            def stage_a(u, mask=mask, prepd=prepd):
                t, bi, hp = u
                qt, kt, kh, eb, vjt, vbt = prepd[(t, hp)]
                g0 = t * TT + bi * 128
                bc = slice(bi * 128, (bi + 1) * 128)
                cols = slice(hp * 128, (hp + 1) * 128)
                tp = r_tp.get()
                for c in range(NCB):
                    self.tr(tp[0:CH, c, :], kh[:, bi * 128 + c * CH:bi * 128 + (c + 1) * CH], self.ident[:], [kh, self.ident], [tp])
                kht = r_kht.get()
                self.cp("act", kht[:], tp[0:CH, :, :], [tp], [kht])
                pat = r_pat.get()
                for h2 in range(2):
                    self.mm(pat[:, h2 * 128:(h2 + 1) * 128], kt[h2][:, bc], qt[:, bc], True, True, [kt[h2], qt], [pat])
                atm = r_atm.get()
                self.tt("dve", atm[:], pat[:, 0:256], mask[:], ALU.mult, [pat, mask], [atm])
                pio = r_pio.get()
                self.mm(pio[:, 0:256], vbt[:, bi, :], atm[:], True, True, [vbt, atm], [pio])
                pu = r_pu.get()
                for c in range(NCB):
                    self.mm(pu[:, c * 128:(c + 1) * 128], kht[0:CH, c, :], vjt[0:CH, bi * NCB + c, :], True, True, [kht, vjt], [pu])
                return pio, pu

            def stage_b(u, pio, pu, fwd=fwd, di=di, cur=cur, curS=curS, prepd=prepd):
                t, bi, hp = u
                qt, kt, kh, eb, vjt, vbt = prepd[(t, hp)]
                g0 = t * TT + bi * 128
                for c in (range(NCB) if fwd else range(NCB - 1, -1, -1)):
                    cc = slice(bi * 128 + c * CH, bi * 128 + (c + 1) * CH)
                    self.mm(pio[:, 256 + c * CH:256 + (c + 1) * CH], cur[hp][:], qt[:, cc], True, True, [cur[hp], qt], [pio])
                    dcol = bi * 128 + c * CH + di
                    sn = r_sst[hp].get()
                    self.stt("dve", sn[:], curS[hp][:], eb[:, dcol:dcol + 1], pu[:, c * 128:(c + 1) * 128], ALU.mult, ALU.add,
                             [curS[hp], eb, pu], [sn])
                    curS[hp] = sn
                    nxt = r_sbf[hp].get()
                    self.tt("pool", nxt[:], sn[:], self.bd_f[:], ALU.mult, [sn, self.bd_f], [nxt])
                    cur[hp] = nxt
                isb = r_isb.get()
                self.cp("act", isb[:], pio[:, 256:384], [pio], [isb])
                gs = slice(g0, g0 + 128)
                if fwd:
                    self.tt("dve", oacc[hp][0:64, gs], pio[0:64, 0:128], isb[0:64, :], ALU.add, [pio, isb], [oacc[hp]])
                    self.tt("dve", oacc[hp][64:128, gs], pio[64:128, 128:256], isb[64:128, :], ALU.add, [pio, isb], [oacc[hp]])
                else:
                    self.tt("dve", oaccb[hp][0:64, gs], pio[0:64, 0:128], isb[0:64, :], ALU.add, [pio, isb], [oaccb[hp]])
                    self.tt("dve", oaccb[hp][64:128, gs], pio[64:128, 128:256], isb[64:128, :], ALU.add, [pio, isb], [oaccb[hp]])

            units = []
            for t in (range(NT) if fwd else range(NT - 1, -1, -1)):
                for bi in (range(NB) if fwd else range(NB - 1, -1, -1)):
                    for hp in range(2):
                        units.append((t, bi, hp))
            prep(units[0][0])
            pend = stage_a(units[0])
            for i, u in enumerate(units):
                nxt_pend = None
                if i + 1 < len(units):
                    if units[i + 1][0] != u[0]:
                        prep(units[i + 1][0])
                    nxt_pend = stage_a(units[i + 1])
                stage_b(u, *pend)
                pend = nxt_pend
        r_sq = self.rot("sqb", [128, TT], BF16, 2)
        r_rs = self.rot("rs", [128, TT], F32, 2)
        r_sg = self.rot("sgt", [128, TT], BF16, 2)
        yb = [self.rot(f"yb{hp}", [128, TT], F32, 2) for hp in range(2)]
        r_ob = self.rot("ob", [128, TT], BF16, 4)
        for t in range(NT):
            if HG_STAGE < 6:
                break
            ys = []
            for hp in range(2):
                o = oacc[hp][:, sl_t(t)]
                self.tt("pool", o, o, oaccb[hp][:, sl_t(t)], ALU.add, [oacc[hp], oaccb[hp]], [oacc[hp]])
                sq = r_sq.get()
                self.act(sq[:], o, AF.Square, [oacc[hp]], [sq])
                pss = r_pat.get()
                self.mm(pss[:, :TT], self.bd[:], sq[:], True, True, [self.bd, sq], [pss])
                rs = r_rs.get()
                self.rstd(rs[:], pss[:, :TT], 64, [pss], [rs])
                sg = r_sg.get()
                self.P.dma("sp", sg[:], self.scr["sg"][hp * 128:(hp + 1) * 128, sl_t(t)], [self.dbufs["sg"][t]], [sg.b])
                y = yb[hp].get()
                self.stt("dve", y[:], o, self.vc(("onorm", l), hp), rs[:], ALU.mult, ALU.mult, [oacc[hp], rs, self.vecs], [y])
                self.tt("dve", y[:], y[:], sg[:], ALU.mult, [y, sg], [y])
                ys.append((y, y[:]))
            self.group_norm_store(ys, [self.vc(("gbr", l), 2), self.vc(("gbr", l), 3)], 256, t, r_pat.get(), r_sq,
                                  r_rs.get(), r_ob)
        self.pop()

    def store_x(self, xt, t):
        TT = self.TT
        dst = self.y.rearrange("(c p) s -> p c s", p=128)[:, :, t * TT:(t + 1) * TT]
        self.P.dma("pool", dst, xt[:], [xt.b], [self.dbufs["y"][t]])

    def ph_outproj(self, l, xsrc, xname):
        TT, NT = self.TT, self.NT
        self.push()
        w = [self.sb(f"w_out{kc}", [128, D], BF16) for kc in range(KC)]
        for kc in range(KC):
            self.P.dma("pool", w[kc][:], self.w_out[l, kc * 128:(kc + 1) * 128, :], [self.wbuf], [w[kc].b])
        r_xt = self.rot("xt", [128, KC, TT], F32, 2)
        r_cat = self.rot("cat", [128, KC, TT], BF16, 2)
        r_ps = self.rot("ps", [128, 512], F32, 4, psum=True)
        for t in range(NT):
            xt, cat = r_xt.get(), r_cat.get()
            self.load_x(xsrc, xname, t, xt)
            self.P.dma("sp", cat[:], self.scr["cat"].rearrange("(c p) s -> p c s", p=128)[:, :, t * TT:(t + 1) * TT],
                       [self.dbufs["cat"][t]], [cat.b])
            for m in range(KC):
                ps = r_ps.get()
                for kc in range(KC):
                    self.mm(ps[:, :TT], w[kc][:, m * 128:(m + 1) * 128], cat[:, kc, :], kc == 0, kc == KC - 1, [w[kc], cat], [ps])
                self.tt("dve", xt[:, m, :], xt[:, m, :], ps[:, :TT], ALU.add, [xt, ps], [xt])
            self.store_x(xt, t)
        self.pop()

    def ph_xattn(self, l, xsrc=None, xname=None):
        TT, NT = self.TT, self.NT
        M = NMEM
        self.push()
        fuse = xsrc is not None
        if fuse:
            wout = [self.sb(f"w_out{kc}", [128, D], BF16) for kc in range(KC)]
            for kc in range(KC):
                self.P.dma("pool", wout[kc][:], self.w_out[l, kc * 128:(kc + 1) * 128, :], [self.wbuf], [wout[kc].b])
            r_cat = self.rot("cat", [128, KC, TT], BF16, 2)
        wq = [self.sb(f"xwq{kc}", [128, 256], BF16) for kc in range(KC)]
        wk = self.sb("xwk", [128, KC, 4, 64], BF16)
        wv = self.sb("xwv", [128, KC, 4, 64], BF16)
        wo = self.sb("xwo", [64, 4, D], BF16)
        kv5 = self.x_wkv[l].rearrange("(c p) (h x) -> p c h x", p=128, h=4)
        for kc in range(KC):
            self.P.dma("pool", wq[kc][:], self.x_wq[l, kc * 128:(kc + 1) * 128, :], [self.wbuf], [wq[kc].b])
        for kc in range(KC):
            self.P.dma("pool", wk[:, kc, :, :], kv5[:, kc, :, 0:64], [self.wbuf], [wk.b])
            self.P.dma("pool", wv[:, kc, :, :], kv5[:, kc, :, 64:128], [self.wbuf], [wv.b])
        self.P.dma("pool", wo[:], self.x_wo[l].rearrange("(h p) n -> p h n", p=64), [self.wbuf], [wo.b])
        R = {"raw": self.rot("raw", [64, TT], F32, 4), "sqh": self.rot("sqh", [64, TT], BF16, 4),
             "pst": self.rot("pst", [128, 512], F32, 2, psum=True), "rsh": self.rot("rsh", [64, TT], F32, 4),
             "rd": self.rot("rd", [65, TT], F32, 2), "osb": self.rot("osb", [64, TT], F32, 2)}
        r_ps = self.rot("ps", [128, 512], F32, 4, psum=True)
        r_pso = self.rot("pso", [128, 512], F32, 2, psum=True)
        mt = self.sb("memT", [128, KC, M], F32)
        self.P.dma("sp", mt[:], self.memT.rearrange("(c p) m -> p c m", p=128), [], [mt.b])
        msq = self.sb("msq", [128, KC, M], BF16)
        self.act(msq[:], mt[:], AF.Square, [mt], [msq])
        pss = R["pst"].get()
        for c in range(KC):
            self.mm(pss[:, :M], self.ones[:], msq[:, c, :], c == 0, c == KC - 1, [self.ones, msq], [pss])
        mrs = self.sb("mrs", [128, M], F32)
        self.rstd(mrs[:], pss[:, :M], D, [pss], [mrs])
        mn = self.sb("mn", [128, KC, M], BF16)
        for c in range(KC):
            self.stt("dve", mn[:, c, :], mt[:, c, :], self.vc(("g_mem", l), c), mrs[:], ALU.mult, ALU.mult, [mt, mrs, self.vecs], [mn])
        KmT = [self.sb(f"KmT{h}", [64, M], BF16) for h in range(4)]
        Vm = self.sb("Vm", [128, 2, 4, 65], BF16)
        self.ms("pool", Vm[:, :, :, 64:65], 1.0, [Vm])
        for h in range(4):
            psk = r_ps.get()
            for c in range(KC):
                self.mm(psk[0:64, :M], wk[:, c, h, :], mn[:, c, :], c == 0, c == KC - 1, [wk, mn], [psk])
            self.head_norm_rope(psk, 64, self.vc(("xkn", l), 0, 0, 64), None, None, KmT[h][0:64, :], KmT[h], R, W=M)
        for blk in range(2):
            psv = r_ps.get()
            for c in range(KC):
                self.mm(psv[:, 0:256], mn[:, c, blk * 128:(blk + 1) * 128], wv[:, c, :, :].rearrange("p h e -> p (h e)"),
                        c == 0, c == KC - 1, [wv, mn], [psv])
            self.cp("act", Vm[:, blk, :, 0:64], psv[:, 0:256].rearrange("p (h e) -> p h e", h=4), [psv], [Vm])
        r_xt = self.rot("xt", [128, KC, TT], F32, 2)
        r_sq = self.rot("sq", [128, KC, TT], BF16, 1)
        r_rs = self.rot("rs", [128, TT], F32, 2)
        r_nT = self.rot("nT", [128, KC, TT], BF16, 1)
        r_q = self.rot("Qx", [64, TT], BF16, 5)
        r_pe = self.rot("pe", [128, TT], BF16, 3)
        yX = [self.sb(f"yX{h}", [64, TT], F32) for h in range(4)]
        oX = [self.sb(f"oX{h}", [64, TT], BF16) for h in range(4)]
        sc = 64.0 ** -0.5
        for t in range(NT):
            xt = r_xt.get()
            if fuse:
                cat = r_cat.get()
                self.load_x(xsrc, xname, t, xt)
                self.P.dma("sp", cat[:], self.scr["cat"].rearrange("(c p) s -> p c s", p=128)[:, :, t * TT:(t + 1) * TT],
                           [self.dbufs["cat"][t]], [cat.b])
                for m in range(KC):
                    ps = r_ps.get()
                    for kc in range(KC):
                        self.mm(ps[:, :TT], wout[kc][:, m * 128:(m + 1) * 128], cat[:, kc, :], kc == 0, kc == KC - 1, [wout[kc], cat], [ps])
                    self.tt("dve", xt[:, m, :], xt[:, m, :], ps[:, :TT], ALU.add, [xt, ps], [xt])
            else:
                self.load_x(self.y, "y", t, xt)
            sq, rs, nT = r_sq.get(), r_rs.get(), r_nT.get()
            self.norm_tile(xt, "g_xq", l, sq, rs, nT, R["pst"].get())
            Qs, psqs = [], []
            for h in range(4):
                psq = r_ps.get()
                for c in range(KC):
                    self.mm(psq[0:64, :TT], wq[c][:, h * 64:(h + 1) * 64], nT[:, c, :], c == 0, c == KC - 1, [wq[c], nT], [psq])
                psqs.append(psq)
                Qs.append(r_q.get())
            self.hnr_multi(psqs, 64, self.vc(("xqn", l), 0, 0, 64), None, None, [(Qs[h][0:64, :], Qs[h]) for h in range(4)], R, r_ps)
            for h in range(4):
                Qx = Qs[h]
                pso = r_pso.get()
                pscs = []
                for blk in range(2):
                    pssc = r_ps.get()
                    self.mm(pssc[:, :TT], KmT[h][0:64, blk * 128:(blk + 1) * 128], Qx[0:64, :], True, True, [KmT[h], Qx], [pssc])
                    pscs.append(pssc)
                for blk in range(2):
                    pssc = pscs[blk]
                    pe = r_pe.get()
                    self.act(pe[:], pssc[:, :TT], AF.Exp, [pssc], [pe], scale=sc)
                    self.mm(pso[0:65, :TT], Vm[:, blk, h, :], pe[:], blk == 0, blk == 1, [Vm, pe], [pso])
                self.softmax_av_finish(pso, h, R, yX)
                self.cp("pool", oX[h][:], yX[h][:], [yX[h]], [oX[h]])
            for m in range(KC):
                ps = r_ps.get()
                for h in range(4):
                    self.mm(ps[:, :TT], wo[0:64, h, m * 128:(m + 1) * 128], oX[h][0:64, :], h == 0, h == 3, [wo, oX[h]], [ps])
                self.tt("dve", xt[:, m, :], xt[:, m, :], ps[:, :TT], ALU.add, [xt, ps], [xt])
            self.store_x(xt, t)
        self.pop()

    def ph_ffn(self, l):
        TT, NT = self.TT, self.NT
        self.push()
        JG = [(0, 6), (6, 12), (12, 17), (17, 22)]
        jgrp = {}
        for gi, (j0, j1) in enumerate(JG):
            for j in range(j0, j1):
                jgrp[j] = (gi, j - j0)
        w13 = [[self.sb(f"w13_{kc}_{gi}", [128, 2, (j1 - j0) * 128], BF16) for gi, (j0, j1) in enumerate(JG)] for kc in range(KC)]
        for gi, (j0, j1) in enumerate(JG):
            for kc in range(KC):
                src = self.f_w13[l, kc * 128:(kc + 1) * 128, :].rearrange("p (two n) -> p two n", two=2)[:, :, j0 * 128:j1 * 128]
                self.P.dma("pool", w13[kc][gi][:], src, [self.wbuf], [w13[kc][gi].b])
        w2v = self.f_w2[l].rearrange("(j p) n -> p j n", p=128)
        w2 = [self.sb(f"w2_{j}", [128, D], BF16) for j in range(NJ)]
        for j in range(NJ):
            self.P.dma("pool", w2[j][:], w2v[:, j, :], [self.wbuf], [w2[j].b])
        r_xt = self.rot("xt", [128, KC, TT], F32, 2)
        r_rs = self.rot("rs", [128, TT], F32, 1)
        r_nT = self.rot("nT", [128, KC, TT], BF16, 1)
        hid = self.sb("hid", [128, NJ, TT], BF16)
        r_s = self.rot("silu", [128, TT], F32, 2)
        r_ps = self.rot("ps", [128, 512], F32, 7, psum=True)
        r_pst = self.rot("pst", [128, 512], F32, 1, psum=True)
        for t in range(NT):
            xt = r_xt.get()
            self.load_x(self.y, "y", t, xt)
            rs, nT = r_rs.get(), r_nT.get()
            self.act(hid[:, 0:KC, :], xt[:], AF.Square, [xt], [hid])
            pst = r_pst.get()
            for c in range(KC):
                self.mm(pst[:, :TT], self.ones[:], hid[:, c, :], c == 0, c == KC - 1, [hid, self.ones], [pst])
            self.rstd(rs[:], pst[:, :TT], D, [pst], [rs])
            for c in range(KC):
                self.stt("dve", nT[:, c, :], xt[:, c, :], self.vc(("g_ffn", l), c), rs[:], ALU.mult, ALU.mult,
                         [xt, rs, self.vecs], [nT])
            for j in range(NJ):
                p1, p3 = r_ps.get(), r_ps.get()
                gi, jj = jgrp[j]
                for kc in range(KC):
                    self.mm(p1[:, :TT], w13[kc][gi][:, 0, jj * 128:(jj + 1) * 128], nT[:, kc, :], kc == 0, kc == KC - 1,
                            [w13[kc][gi], nT], [p1])
                for kc in range(KC):
                    self.mm(p3[:, :TT], w13[kc][gi][:, 1, jj * 128:(jj + 1) * 128], nT[:, kc, :], kc == 0, kc == KC - 1,
                            [w13[kc][gi], nT], [p3])
                s = r_s.get()
                self.act(s[:], p1[:, :TT], AF.Silu, [p1], [s])
                self.tt("dve", hid[:, j, :], s[:], p3[:, :TT], ALU.mult, [s, p3], [hid])
            for m in range(KC):
                ps = r_ps.get()
                for j in range(NJ):
                    self.mm(ps[:, :TT], w2[j][:, m * 128:(m + 1) * 128], hid[:, j, :], j == 0, j == NJ - 1,
                            [w2[j], hid], [ps])
                self.tt("dve", xt[:, m, :], xt[:, m, :], ps[:, :TT], ALU.add, [xt, ps], [xt])
            self.store_x(xt, t)
        self.pop()

    def build(self, phases="all"):
        on = lambda p: phases == "all" or p in phases.split(",")
        self.setup()
        for l in range(self.L):
            xsrc, xname = (self.xin, "xin") if l == 0 else (self.y, "y")
            if on("inproj"):
                self.ph_inproj(l, xsrc, xname)
            if on("convA"):
                self.ph_conv(l, "A")
            if on("hgrn"):
                self.ph_hgrn(l)
            if on("convC"):
                self.ph_conv(l, "C")
            if on("mla"):
                self.ph_mla(l)
            if on("outproj") and on("xattn"):
                self.ph_xattn(l, xsrc, xname)
            else:
                if on("outproj"):
                    self.ph_outproj(l, xsrc, xname)
                if on("xattn"):
                    self.ph_xattn(l)
            if on("ffn"):
                self.ph_ffn(l)
        self.P.finish_wait_all(self.dbufs["y"])
        self.pop()
        self.P.close()
        return self.nc


def host_inputs(inp, L, S):
    B = inp["x"].shape[0]
    cm = const_mats()
    vecs = pack_vecs(inp, L)
    shared = {"vecs": vecs}
    for k, v in cm.items():
        shared["c_" + k] = v
    for k in ["w_in", "m_wuq", "m_wukv", "w_out", "x_wq", "x_wkv", "x_wo", "f_w13", "f_w2"]:
        shared[k] = np.ascontiguousarray(np.asarray(inp[k], np.float32)[:L])
    maps = []
    for b in range(B):
        m = dict(shared)
        m["xT"] = np.ascontiguousarray(np.asarray(inp["x"][b], np.float32).T)
        m["memT"] = np.ascontiguousarray(np.asarray(inp["mem"][b], np.float32).T)
        m["pos"] = np.ascontiguousarray(np.asarray(inp["positions"][b], np.int32).reshape(1, S))
        maps.append(m)
    return maps


def kernel(**inputs):
    L = inputs["w_in"].shape[0]
    B, S, _ = inputs["x"].shape
    kb = KB(S, L, TT=512)
    nc = kb.build()
    maps = host_inputs(inputs, L, S)
    res = run_bass_kernel_spmd(nc, maps, core_ids=list(range(B)))
    out = np.stack([np.asarray(r["y"], np.float32).T for r in res.results], axis=0)
    return np.ascontiguousarray(out)
```
